# Optimizing a Trainium2 kernel written in Bass

```python
import math
import jax
import jax.numpy as jnp
from jax import lax
import numpy as np

D_MODEL = 4096
BATCH = 4
SEQ = 4096
DEPTH = 2
DEC_BATCH = 8
DEC_SEQ = 2048
PAST_LEN = 128

EPS = 1e-6
NEG_INF = -1e30
N_BRANCH = 4
BRANCH_W = D_MODEL // N_BRANCH
HEAD_DIM = 128
Q_BLOCK = 128

MLA_NOPE = 128
MLA_ROPE = 64
MLA_V = 128
MLA_HEADS = BRANCH_W // MLA_V
MLA_Q_LORA = D_MODEL // 4
MLA_KV_LORA = D_MODEL // 8
ROPE_THETA = 10000.0

DIL_HEADS = BRANCH_W // HEAD_DIM
DIL_PATTERNS = ((128, 1), (512, 4), (2048, 16))

WIN_Q_HEADS = BRANCH_W // HEAD_DIM
WIN_KV_HEADS = WIN_Q_HEADS // 4
WIN_RADIUS = 128

DIFF_HD = 128
DIFF_HEADS = BRANCH_W // (2 * DIFF_HD)

MEM_TOKENS = 256
MEM_HEADS = 4
MEM_W = MEM_HEADS * HEAD_DIM

FFN_HIDDEN = -(-8 * D_MODEL // (3 * 256)) * 256

MIX_SPLITS = (MLA_Q_LORA, MLA_KV_LORA, MLA_ROPE,
              DIL_HEADS * HEAD_DIM, DIL_HEADS * HEAD_DIM, DIL_HEADS * HEAD_DIM,
              WIN_Q_HEADS * HEAD_DIM, WIN_KV_HEADS * HEAD_DIM, WIN_KV_HEADS * HEAD_DIM,
              DIFF_HEADS * 2 * DIFF_HD, DIFF_HEADS * 2 * DIFF_HD, DIFF_HEADS * 2 * DIFF_HD)
MIX_COLS = sum(MIX_SPLITS)
MIX_OFFSETS = tuple(int(o) for o in np.cumsum(MIX_SPLITS)[:-1])
IN_COLS = MIX_COLS + N_BRANCH * D_MODEL

kernel_name = 'hybrid_gated_mla_dilated_window_diff_encoder'


def rms_norm(x, g):
    xf = x.astype(jnp.float32)
    y = xf * lax.rsqrt(jnp.mean(xf * xf, axis=-1, keepdims=True) + EPS)
    return (y * g.astype(jnp.float32)).astype(x.dtype)


def alibi_slopes(n):
    return 2.0 ** (-8.0 * jnp.arange(1, n + 1, dtype=jnp.float32) / n)


def rotary(x, pos):
    half = x.shape[-1] // 2
    inv_freq = ROPE_THETA ** (-jnp.arange(half, dtype=jnp.float32) / half)
    ang = pos.astype(jnp.float32)[:, None] * inv_freq[None, :]
    cos = jnp.cos(ang)[:, None, :]
    sin = jnp.sin(ang)[:, None, :]
    xf = x.astype(jnp.float32)
    x1, x2 = xf[..., :half], xf[..., half:]
    return jnp.concatenate([x1 * cos - x2 * sin, x2 * cos + x1 * sin], axis=-1).astype(x.dtype)


def dense_attention(q, k, v):
    B, S, H, dk = q.shape
    nb = S // Q_BLOCK
    scale = dk ** -0.5
    qb = q.reshape(B, nb, Q_BLOCK, H, dk).swapaxes(0, 1)

    def block(qi):
        s = jnp.einsum('bqhd,bkhd->bhqk', qi, k, preferred_element_type=jnp.float32) * scale
        p = jax.nn.softmax(s, axis=-1)
        return jnp.einsum('bhqk,bkhd->bqhd', p.astype(v.dtype), v)

    o = lax.map(block, qb)
    return o.swapaxes(0, 1).reshape(B, S, H, v.shape[-1])


def differential_attention(q, k, v, slopes, lam):
    B, S, H, _, dh = q.shape
    nb = S // Q_BLOCK
    scale = dh ** -0.5
    qb = q.reshape(B, nb, Q_BLOCK, H, 2, dh).swapaxes(0, 1)
    kpos = jnp.arange(S)

    def block(args):
        qi, i = args
        qpos = i * Q_BLOCK + jnp.arange(Q_BLOCK)
        dist = jnp.abs(qpos[:, None] - kpos[None, :]).astype(jnp.float32)
        s = jnp.einsum('bqhmd,bkhmd->bhmqk', qi, k, preferred_element_type=jnp.float32) * scale
        s = s - slopes[None, :, None, None, None] * dist
        p = jax.nn.softmax(s, axis=-1)
        a = p[:, :, 0] - lam * p[:, :, 1]
        return jnp.einsum('bhqk,bkhd->bqhd', a.astype(v.dtype), v)

    o = lax.map(block, (qb, jnp.arange(nb)))
    return o.swapaxes(0, 1).reshape(B, S, H, v.shape[-1])


def banded_attention(q, k, v, radius, slopes, dist_scale, sink=None):
    N, L, H, dh = q.shape
    G = k.shape[2]
    rep = H // G
    blk = radius
    nb = -(-L // blk)
    pad = nb * blk - L
    scale = dh ** -0.5
    qb = jnp.pad(q, ((0, 0), (0, pad), (0, 0), (0, 0))).reshape(N, nb, blk, G, rep, dh)

    def key_windows(t):
        tb = jnp.pad(t, ((0, 0), (blk, blk + pad), (0, 0), (0, 0))).reshape(N, nb + 2, blk, G, dh)
        return jnp.concatenate([tb[:, :-2], tb[:, 1:-1], tb[:, 2:]], axis=2)

    kw, vw = key_windows(k), key_windows(v)
    a = jnp.arange(blk)
    c = jnp.arange(3 * blk)
    delta = c[None, :] - blk - a[:, None]
    kpos = jnp.arange(nb)[:, None] * blk + c[None, :] - blk
    valid = (jnp.abs(delta)[None] <= radius) & (kpos[:, None, :] >= 0) & (kpos[:, None, :] < L)
    bias = -(slopes * dist_scale).reshape(G, rep, 1, 1) * jnp.abs(delta).astype(jnp.float32)
    s = jnp.einsum('nbqgrd,nbkgd->nbgrqk', qb, kw, preferred_element_type=jnp.float32) * scale + bias
    s = jnp.where(valid[None, :, None, None], s, NEG_INF)
    m = jnp.max(s, axis=-1)
    if sink is not None:
        sk = sink.astype(jnp.float32).reshape(1, 1, G, rep, 1)
        m = jnp.maximum(m, sk)
    p = jnp.exp(s - m[..., None])
    denom = jnp.sum(p, axis=-1)
    if sink is not None:
        denom = denom + jnp.exp(sk - m)
    o = jnp.einsum('nbgrqk,nbkgd->nbqgrd', (p / denom[..., None]).astype(v.dtype), vw)
    lse = (m + jnp.log(denom)).transpose(0, 1, 4, 2, 3)
    o = o.reshape(N, nb * blk, H, dh)[:, :L]
    lse = lse.reshape(N, nb * blk, H)[:, :L]
    return o, lse


def mla_branch(q_lat, kv_lat, k_rope, pos, qa_g, kva_g, wq_up, wkv_up, qk_g):
    B, S, _ = q_lat.shape
    q = (rms_norm(q_lat, qa_g) @ wq_up).reshape(B, S, MLA_HEADS, MLA_NOPE + MLA_ROPE)
    kv = (rms_norm(kv_lat, kva_g) @ wkv_up).reshape(B, S, MLA_HEADS, MLA_NOPE + MLA_V)
    k_pe = rotary(k_rope.reshape(B, S, 1, MLA_ROPE), pos)
    q = jnp.concatenate([q[..., :MLA_NOPE], rotary(q[..., MLA_NOPE:], pos)], axis=-1)
    k = jnp.concatenate([kv[..., :MLA_NOPE],
                         jnp.broadcast_to(k_pe, (B, S, MLA_HEADS, MLA_ROPE))], axis=-1)
    v = kv[..., MLA_NOPE:]
    o = dense_attention(rms_norm(q, qk_g[0]), rms_norm(k, qk_g[1]), v)
    return o.reshape(B, S, BRANCH_W)


def to_residue(t, dil):
    B, S, H, d = t.shape
    return t.reshape(B, S // dil, dil, H, d).transpose(0, 2, 1, 3, 4).reshape(B * dil, S // dil, H, d)


def from_residue(t, B, dil):
    N, L = t.shape[:2]
    rest = t.shape[2:]
    t = t.reshape((B, dil, L) + rest)
    t = jnp.moveaxis(t, 1, 2)
    return t.reshape((B, L * dil) + rest)


def dilated_branch(q, k, v, qk_g, slopes):
    B, S, _ = q.shape
    shp = (B, S, DIL_HEADS, HEAD_DIM)
    q = rms_norm(q.reshape(shp), qk_g[0])
    k = rms_norm(k.reshape(shp), qk_g[1])
    v = v.reshape(shp)
    outs, lses = [], []
    for window, dil in DIL_PATTERNS:
        o, lse = banded_attention(to_residue(q, dil), to_residue(k, dil), to_residue(v, dil),
                                  window // (2 * dil), slopes, float(dil))
        outs.append(from_residue(o, B, dil))
        lses.append(from_residue(lse, B, dil))
    w = jax.nn.softmax(jnp.stack(lses), axis=0)
    o = jnp.einsum('pbsh,pbshd->bshd', w, jnp.stack(outs).astype(jnp.float32))
    return o.astype(q.dtype).reshape(B, S, BRANCH_W)


def window_branch(q, k, v, qk_g, sink, slopes):
    B, S, _ = q.shape
    q = rms_norm(q.reshape(B, S, WIN_Q_HEADS, HEAD_DIM), qk_g[0])
    k = rms_norm(k.reshape(B, S, WIN_KV_HEADS, HEAD_DIM), qk_g[1])
    v = v.reshape(B, S, WIN_KV_HEADS, HEAD_DIM)
    o, _ = banded_attention(q, k, v, WIN_RADIUS, slopes, 1.0, sink)
    return o.reshape(B, S, BRANCH_W)


def diff_branch(q, k, v, qk_g, lam_p, subln_g, slopes, layer):
    B, S, _ = q.shape
    lam_init = 0.8 - 0.6 * math.exp(-0.3 * layer)
    lp = lam_p.astype(jnp.float32)
    lam = jnp.exp(jnp.sum(lp[0] * lp[1])) - jnp.exp(jnp.sum(lp[2] * lp[3])) + lam_init
    q = rms_norm(q.reshape(B, S, DIFF_HEADS, 2, DIFF_HD), qk_g[0])
    k = rms_norm(k.reshape(B, S, DIFF_HEADS, 2, DIFF_HD), qk_g[1])
    v = v.reshape(B, S, DIFF_HEADS, 2 * DIFF_HD)
    o = differential_attention(q, k, v, slopes, lam)
    o = rms_norm(o, subln_g) * (1.0 - lam_init)
    return o.reshape(B, S, BRANCH_W)


def memory_attention(x, mem, ln_g, mem_ln_g, wq, wkv, wo, qk_g):
    B, S, _ = x.shape
    M = mem.shape[1]
    q = rms_norm((rms_norm(x, ln_g) @ wq).reshape(B, S, MEM_HEADS, HEAD_DIM), qk_g[0])
    kv = (rms_norm(mem, mem_ln_g) @ wkv).reshape(B, M, 2, MEM_HEADS, HEAD_DIM)
    k = rms_norm(kv[:, :, 0], qk_g[1])
    v = kv[:, :, 1]
    s = jnp.einsum('bshd,bmhd->bhsm', q, k, preferred_element_type=jnp.float32) * HEAD_DIM ** -0.5
    p = jax.nn.softmax(s, axis=-1)
    o = jnp.einsum('bhsm,bmhd->bshd', p.astype(v.dtype), v).reshape(B, S, MEM_W)
    return o @ wo


def swiglu(x, ln_g, w_in, w_out):
    g, u = jnp.split(rms_norm(x, ln_g) @ w_in, 2, axis=-1)
    return (jax.nn.silu(g) * u) @ w_out


def trunk(x, mem, ln_mix_g, w_in, mla_qa_g, mla_kva_g, mla_wq_up, mla_wkv_up, mla_qk_g,
          dil_qk_g, win_qk_g, win_sink, diff_qk_g, diff_lambda, diff_subln_g, w_branch, w_out,
          ln_mem_g, mem_ln_g, mem_wq, mem_wkv, mem_qk_g, mem_wo, ln_ffn_g, ffn_w_in, ffn_w_out):
    S = x.shape[1]
    pos = jnp.arange(S)
    slopes_dil = alibi_slopes(DIL_HEADS)
    slopes_win = alibi_slopes(WIN_Q_HEADS)
    slopes_diff = alibi_slopes(DIFF_HEADS)
    for l in range(DEPTH):
        h = rms_norm(x, ln_mix_g[l])
        (qa, kva, kpe, dq, dk, dv, wq, wk, wv, fq, fk, fv) = jnp.split(
            h @ w_in[l, :, :MIX_COLS], MIX_OFFSETS, axis=-1)
        branches = (
            mla_branch(qa, kva, kpe, pos, mla_qa_g[l], mla_kva_g[l], mla_wq_up[l], mla_wkv_up[l], mla_qk_g[l]),
            dilated_branch(dq, dk, dv, dil_qk_g[l], slopes_dil),
            window_branch(wq, wk, wv, win_qk_g[l], win_sink[l], slopes_win),
            diff_branch(fq, fk, fv, diff_qk_g[l], diff_lambda[l], diff_subln_g[l], slopes_diff, l),
        )
        merged = jnp.zeros_like(x)
        for i, o in enumerate(branches):
            g0 = MIX_COLS + i * D_MODEL
            gate = jax.nn.sigmoid(h @ w_in[l, :, g0:g0 + D_MODEL])
            merged = merged + gate * (o @ w_branch[l, i])
        x = x + merged @ w_out[l]
        x = x + memory_attention(x, mem, ln_mem_g[l], mem_ln_g[l], mem_wq[l], mem_wkv[l], mem_wo[l], mem_qk_g[l])
        x = x + swiglu(x, ln_ffn_g[l], ffn_w_in[l], ffn_w_out[l])
    return x


def setup_inputs(seed: int = 0) -> dict:
    key = jax.random.key(seed)
    ks = jax.random.split(key, 32)

    def nrm(k, shape, scale):
        return jax.random.normal(k, shape, jnp.float32) * scale

    def gain(k, shape):
        return 1.0 + 0.02 * jax.random.normal(k, shape, jnp.float32)

    return {
        'x_prompt': nrm(ks[0], (BATCH, SEQ, D_MODEL), 1.0),
        'x_sample': nrm(ks[1], (DEC_BATCH, DEC_SEQ, D_MODEL), 1.0),
        'mem_prompt': nrm(ks[2], (BATCH, MEM_TOKENS, D_MODEL), 1.0),
        'mem_sample': nrm(ks[3], (DEC_BATCH, MEM_TOKENS, D_MODEL), 1.0),
        'ln_mix_g': gain(ks[4], (DEPTH, D_MODEL)),
        'w_in': nrm(ks[5], (DEPTH, D_MODEL, IN_COLS), D_MODEL ** -0.5),
        'mla_qa_g': gain(ks[6], (DEPTH, MLA_Q_LORA)),
        'mla_kva_g': gain(ks[7], (DEPTH, MLA_KV_LORA)),
        'mla_wq_up': nrm(ks[8], (DEPTH, MLA_Q_LORA, MLA_HEADS * (MLA_NOPE + MLA_ROPE)), MLA_Q_LORA ** -0.5),
        'mla_wkv_up': nrm(ks[9], (DEPTH, MLA_KV_LORA, MLA_HEADS * (MLA_NOPE + MLA_V)), MLA_KV_LORA ** -0.5),
        'mla_qk_g': gain(ks[10], (DEPTH, 2, MLA_NOPE + MLA_ROPE)),
        'dil_qk_g': gain(ks[11], (DEPTH, 2, HEAD_DIM)),
        'win_qk_g': gain(ks[12], (DEPTH, 2, HEAD_DIM)),
        'win_sink': nrm(ks[13], (DEPTH, WIN_Q_HEADS), 0.5),
        'diff_qk_g': gain(ks[14], (DEPTH, 2, DIFF_HD)),
        'diff_lambda': nrm(ks[15], (DEPTH, 4, DIFF_HD), 0.1),
        'diff_subln_g': gain(ks[16], (DEPTH, 2 * DIFF_HD)),
        'w_branch': nrm(ks[17], (DEPTH, N_BRANCH, BRANCH_W, D_MODEL), BRANCH_W ** -0.5),
        'w_out': nrm(ks[18], (DEPTH, D_MODEL, D_MODEL), D_MODEL ** -0.5),
        'ln_mem_g': gain(ks[19], (DEPTH, D_MODEL)),
        'mem_ln_g': gain(ks[20], (DEPTH, D_MODEL)),
        'mem_wq': nrm(ks[21], (DEPTH, D_MODEL, MEM_W), D_MODEL ** -0.5),
        'mem_wkv': nrm(ks[22], (DEPTH, D_MODEL, 2 * MEM_W), D_MODEL ** -0.5),
        'mem_qk_g': gain(ks[23], (DEPTH, 2, HEAD_DIM)),
        'mem_wo': nrm(ks[24], (DEPTH, MEM_W, D_MODEL), MEM_W ** -0.5),
        'ln_ffn_g': gain(ks[25], (DEPTH, D_MODEL)),
        'ffn_w_in': nrm(ks[26], (DEPTH, D_MODEL, 2 * FFN_HIDDEN), D_MODEL ** -0.5),
        'ffn_w_out': nrm(ks[27], (DEPTH, FFN_HIDDEN, D_MODEL), FFN_HIDDEN ** -0.5),
    }


def reference(x_prompt, x_sample, mem_prompt, mem_sample, ln_mix_g, w_in, mla_qa_g, mla_kva_g,
              mla_wq_up, mla_wkv_up, mla_qk_g, dil_qk_g, win_qk_g, win_sink, diff_qk_g, diff_lambda,
              diff_subln_g, w_branch, w_out, ln_mem_g, mem_ln_g, mem_wq, mem_wkv, mem_qk_g, mem_wo,
              ln_ffn_g, ffn_w_in, ffn_w_out):
    params = (ln_mix_g, w_in, mla_qa_g, mla_kva_g, mla_wq_up, mla_wkv_up, mla_qk_g,
              dil_qk_g, win_qk_g, win_sink, diff_qk_g, diff_lambda, diff_subln_g, w_branch, w_out,
              ln_mem_g, mem_ln_g, mem_wq, mem_wkv, mem_qk_g, mem_wo, ln_ffn_g, ffn_w_in, ffn_w_out)
    y_prompt = trunk(x_prompt, mem_prompt, *params)
    y_sample = trunk(x_sample, mem_sample, *params)
    return (y_prompt, y_sample)
```

```python
import math
from contextlib import ExitStack
import numpy as np
import concourse.bass as bass
import concourse.mybir as mybir
from concourse.bass_utils import run_bass_kernel_spmd

F32 = mybir.dt.float32
BF16 = mybir.dt.bfloat16
AF = mybir.ActivationFunctionType
ALU = mybir.AluOpType
EPS = 1e-6
NEG = -1e30


class Cfg:
    def __init__(s, D=4096, TOK=4096, MEM=256, depth=2):
        s.D = D; s.TOK = TOK; s.MEM = MEM; s.depth = depth
        s.NC = D // 128
        s.TS = 512
        s.NST = TOK // s.TS
        s.NCH = TOK // 128
        s.HALF = TOK // 2
        s.BW = D // 4
        s.H = s.BW // 128
        s.WKV = s.H // 4
        s.DH = s.BW // 256
        s.QL = D // 4; s.KVL = D // 8
        s.FH = -(-8 * D // (3 * 256)) * 256
        s.FC = s.FH // 128
        s.MEMH = 4; s.MEMW = 512
        s.MIX = s.QL + s.KVL + 64 + 3 * s.BW + s.BW + 2 * s.WKV * 128 + 3 * s.BW
        s.INC = s.MIX + 4 * D
        o = 0
        s.o_qa = o; o += s.QL
        s.o_kva = o; o += s.KVL
        s.o_kpe = o; o += 64
        s.o_dq = o; o += s.BW
        s.o_dk = o; o += s.BW
        s.o_dv = o; o += s.BW
        s.o_wq = o; o += s.BW
        s.o_wk = o; o += s.WKV * 128
        s.o_wv = o; o += s.WKV * 128
        s.o_fq = o; o += s.BW
        s.o_fk = o; o += s.BW
        s.o_fv = o; o += s.BW
        assert o == s.MIX
        i = 0
        s.q_mla_n = i; i += s.H
        s.k_mla_n = i; i += s.H
        s.q_mla_r = i; i += s.H
        s.k_mla_r = i; i += s.H
        s.q_dil = i; i += s.H
        s.k_dil = i; i += s.H
        s.q_win = i; i += s.H
        s.k_win = i; i += s.WKV
        s.q_dif = i; i += 2 * s.DH
        s.k_dif = i; i += 2 * s.DH
        s.nQK = i
        i = 0
        s.v_mla = i; i += s.H
        s.v_dil = i; i += s.H
        s.v_win = i; i += s.WKV
        s.v_dif = i; i += 2 * s.DH
        s.nV = i
        g = 0
        s.g_lnmix = g; g += s.NC
        s.g_lnmem = g; g += s.NC
        s.g_memln = g; g += s.NC
        s.g_lnffn = g; g += s.NC
        s.g_qa = g; g += s.QL // 128
        s.g_kva = g; g += s.KVL // 128
        s.g_mla = g; g += 4
        s.g_dil = g; g += 2
        s.g_win = g; g += 2
        s.g_dif = g; g += 2
        s.g_mem = g; g += 2
        s.g_subln = g; g += 2
        s.g_lam = g; g += 4
        s.g_sink = g; g += s.H
        s.NG = g


def alibi(n):
    return [2.0 ** (-8.0 * (i + 1) / n) for i in range(n)]


class Buf:
    __slots__ = ("name", "lw", "rd", "dsem", "dcnt", "multi")

    def __init__(s, name, multi=False):
        s.name = name; s.lw = {}; s.rd = {}; s.dsem = None; s.dcnt = 0; s.multi = multi


class KB:
    def __init__(s, nc):
        s.nc = nc
        s.es = ExitStack()
        s.E = {"pe": nc.tensor, "act": nc.scalar, "dve": nc.vector, "pool": nc.gpsimd, "sp": nc.sync}
        s.sem = {e: s.es.enter_context(nc.semaphore("e_" + e)) for e in ("pe", "act", "dve", "pool")}
        s.cnt = {e: 0 for e in s.sem}
        s.waited = {e: {} for e in s.E}
        s.allsems = {}
        s.nsem = 4
        s.sem_pool = []

    def _wait(s, eng, ev, skip=None):
        sem, val = ev
        if eng == "pe" and sem is s.sem["pe"]:
            return
        if skip is not None and sem is skip:
            return
        w = s.waited[eng]
        if w.get(sem.num, 0) < val:
            s.E[eng].wait_ge(sem, val)
            w[sem.num] = val

    def _deps(s, eng, reads, writes, skip=None):
        for b in reads:
            for ev in b.lw.values():
                s._wait(eng, ev, skip)
        for b in writes:
            if not b.multi:
                for ev in b.lw.values():
                    s._wait(eng, ev, skip)
            for ev in b.rd.values():
                s._wait(eng, ev, skip)

    def _commit(s, ev, reads, writes):
        for b in reads:
            b.rd[ev[0].num] = ev
        for b in writes:
            if b.multi and not b.rd:
                b.lw[ev[0].num] = ev
            else:
                b.lw = {ev[0].num: ev}
            b.rd = {}
        s.allsems[ev[0].num] = ev

    def op(s, eng, fn, reads=(), writes=()):
        s._deps(eng, reads, writes)
        ins = fn(s.E[eng])
        s.cnt[eng] += 1
        ins.then_inc(s.sem[eng], 1)
        s._commit((s.sem[eng], s.cnt[eng]), reads, writes)

    def dma(s, q, out, in_, reads, writes, sb):
        if sb.dsem is None:
            if s.sem_pool:
                sb.dsem, sb.dcnt = s.sem_pool.pop()
            else:
                sb.dsem = s.es.enter_context(s.nc.semaphore("d_" + sb.name))
                s.nsem += 1
                sb.dcnt = 0
        s._deps(q, reads, writes, skip=sb.dsem)
        ins = s.E[q].dma_start(out=out, in_=in_)
        sb.dcnt += 16
        ins.then_inc(sb.dsem, 16)
        s._commit((sb.dsem, sb.dcnt), reads, writes)

    def barrier(s):
        for eng in s.E:
            for ev in list(s.allsems.values()):
                s._wait(eng, ev)


class Scope:
    n = 0

    def __init__(s, kb):
        s.kb = kb; s.es = ExitStack(); s.bufs = []
        Scope.n += 1; s.id = Scope.n

    def T(s, name, shape, dt, psum=False):
        return T(s.kb, f"{name}_s{s.id}", shape, dt, psum=psum, scope=s)

    def ring(s, name, n, shape, dt):
        r = Ring.__new__(Ring)
        r.tiles = [s.T(f"{name}{i}", shape, dt) for i in range(n)]; r.i = 0
        return r

    def buf(s, name):
        b = Buf(f"{name}_s{s.id}"); s.bufs.append(b); return b

    def close(s):
        s.kb.barrier()
        for b in s.bufs:
            if b.dsem is not None:
                s.kb.sem_pool.append((b.dsem, b.dcnt)); b.dsem = None
        s.es.close()


class T:
    def __init__(s, kb, name, shape, dt, psum=False, scope=None):
        s.b = Buf(name)
        alloc = kb.nc.psum_tensor if psum else kb.nc.sbuf_tensor
        es = scope.es if scope is not None else kb.es
        s.t = es.enter_context(alloc(name, list(shape), dt))
        if scope is not None:
            scope.bufs.append(s.b)

    def __getitem__(s, k):
        return s.t[k]


class Ring:
    def __init__(s, kb, name, n, shape, dt):
        s.tiles = [T(kb, f"{name}{i}", shape, dt) for i in range(n)]
        s.i = 0

    def next(s):
        t = s.tiles[s.i % len(s.tiles)]
        s.i += 1
        return t


class WSpec:
    def __init__(s, name, kc, bw, blocks):
        s.name = name; s.kc = kc; s.bw = bw; s.blocks = blocks; s.dram = None; s.buf = None


def build_specs(c):
    sp = {}
    D = c.D

    def seg(src, col0, w, sub=None, row0=0):
        return (src, sub, row0, col0, w)

    blocks = []
    kinds = []
    def add_range(o, n, kindf):
        nb = n // 256
        for j in range(nb):
            blocks.append([seg("w_in", o + j * 256, 256)])
            kinds.append([kindf(2 * j), kindf(2 * j + 1)])
    add_range(c.o_qa, c.QL, lambda i: ("latq", i))
    add_range(c.o_kva, c.KVL, lambda i: ("latkv", i))
    blocks.append([seg("w_in", c.o_kpe, 64), seg("w_in", c.o_kpe + 32, 32), seg("w_in", c.o_kpe, 32)])
    kinds.append([("kpe", 0)])
    add_range(c.o_dq, c.BW, lambda i: ("hn", c.q_dil + i, c.g_dil))
    add_range(c.o_dk, c.BW, lambda i: ("hn", c.k_dil + i, c.g_dil + 1))
    add_range(c.o_dv, c.BW, lambda i: ("v", c.v_dil + i))
    add_range(c.o_wq, c.BW, lambda i: ("hn", c.q_win + i, c.g_win))
    if c.WKV * 128 >= 256:
        add_range(c.o_wk, c.WKV * 128, lambda i: ("hn", c.k_win + i, c.g_win + 1))
        add_range(c.o_wv, c.WKV * 128, lambda i: ("v", c.v_win + i))
    else:
        blocks.append([seg("w_in", c.o_wk, 128), seg("w_in", c.o_wv, 128)])
        kinds.append([("hn", c.k_win, c.g_win + 1), ("v", c.v_win)])
    add_range(c.o_fq, c.BW, lambda i: ("hn", c.q_dif + i, c.g_dif))
    add_range(c.o_fk, c.BW, lambda i: ("hn", c.k_dif + i, c.g_dif + 1))
    add_range(c.o_fv, c.BW, lambda i: ("v", c.v_dif + i))
    s = WSpec("wa", c.NC, 256, blocks); s.kinds = kinds
    sp["wa"] = s
    sp["wqu"] = WSpec("wqu", c.QL // 128, 256,
                      [[seg("mla_wq_up", h * 192, 192), seg("mla_wq_up", h * 192 + 160, 32),
                        seg("mla_wq_up", h * 192 + 128, 32)] for h in range(c.H)])
    sp["wkvu"] = WSpec("wkvu", c.KVL // 128, 256, [[seg("mla_wkv_up", h * 256, 256)] for h in range(c.H)])
    gb = []
    for ob in range(c.NC):
        for pr in range(2):
            gb.append([seg("w_in", c.MIX + (2 * pr) * D + ob * 128, 128),
                       seg("w_in", c.MIX + (2 * pr + 1) * D + ob * 128, 128)])
    sp["wg"] = WSpec("wg", c.NC, 256, gb)
    sp["wb"] = WSpec("wb", c.BW // 128, 512,
                     [[seg("w_branch", ob * 128, 128, sub=i) for i in range(4)] for ob in range(c.NC)])
    sp["wo"] = WSpec("wo", c.NC, 256, [[seg("w_out", j * 256, 256)] for j in range(D // 256)])
    sp["mq"] = WSpec("mq", c.NC, 256, [[seg("mem_wq", j * 256, 256)] for j in range(c.MEMW // 256)])
    sp["mkv"] = WSpec("mkv", c.NC, 256, [[seg("mem_wkv", j * 256, 256)] for j in range(2 * c.MEMW // 256)])
    sp["mo"] = WSpec("mo", c.MEMW // 128, 256, [[seg("mem_wo", j * 256, 256)] for j in range(D // 256)])
    sp["fi"] = WSpec("fi", c.NC, 256, [[seg("ffn_w_in", j * 128, 128), seg("ffn_w_in", c.FH + j * 128, 128)]
                                       for j in range(c.FC)])
    c.FCH = (c.FC + 1) // 2
    fo = []
    for hf in range(2):
        k0 = hf * c.FCH
        nk = min(c.FCH, c.FC - k0)
        for ob in range(c.NC):
            fo.append([seg("ffn_w_out", ob * 128, 128, row0=k0 * 128)] + [("__nk__", nk)])
    s = WSpec("fo", c.FCH, 128, fo)
    sp["fo"] = s
    return sp


WSLOT = 8192


def build(c, debug=False):
    nc = bass.Bass("TRN2", target_bir_lowering=False)
    kb = KB(nc)
    D, TOK, TS, NC = c.D, c.TOK, c.TS, c.NC
    L = c.depth

    def din(name, shape, dt=F32):
        return nc.dram_tensor(name, list(shape), dt, kind="ExternalInput").ap()

    def dscr(name, shape, dt):
        kind = "ExternalOutput" if debug else "Internal"
        return nc.dram_tensor(name, list(shape), dt, kind=kind).ap()

    xT = din("xT", [D, TOK]); memT = din("memT", [D, 2 * c.MEM])
    W = {
        "w_in": din("w_in", [L, D, c.INC]), "mla_wq_up": din("mla_wq_up", [L, c.QL, c.H * 192]),
        "mla_wkv_up": din("mla_wkv_up", [L, c.KVL, c.H * 256]), "w_branch": din("w_branch", [L, 4, c.BW, D]),
        "w_out": din("w_out", [L, D, D]), "mem_wq": din("mem_wq", [L, D, c.MEMW]),
        "mem_wkv": din("mem_wkv", [L, D, 2 * c.MEMW]), "mem_wo": din("mem_wo", [L, c.MEMW, D]),
        "ffn_w_in": din("ffn_w_in", [L, D, 2 * c.FH]), "ffn_w_out": din("ffn_w_out", [L, c.FH, D]),
    }
    GT_d = din("gt", [128, L * c.NG])
    cos_d = din("cosT", [64, TOK]); sin_d = din("sinT", [64, TOK])
    bm_d = din("bmask", [128, 2])
    ident_d = din("ident_in", [128, 128])
    WW, WD, WL, WA = tab_widths(c)
    tabw_d = din("tab_win", [128, WW]); tabd_d = din("tab_dil", [128, WD]); tabl_d = din("tab_lm", [128, WL])
    taba_d = din("tab_abs", [128, WA], mybir.dt.int16)
    yT = nc.dram_tensor("yT", [D, TOK], F32, kind="ExternalOutput").ap()

    specs = [build_specs(c) for _ in range(L)]
    for l in range(L):
        for s in specs[l].values():
            s.dram = dscr(f"wb_{s.name}_{l}", [len(s.blocks) * 128, s.kc * s.bw], BF16)
            s.bufs = [Buf(f"wb_{s.name}_{l}_{j}", True) for j in range(len(s.blocks))]
    QK = dscr("qk", [c.nQK * 128, TOK], BF16); QKb = [Buf(f"qk{i}", True) for i in range(c.nQK)]
    VS = dscr("vs", [c.nV * 128, c.NCH * 128], BF16); VSb = [Buf(f"vs{i}", True) for i in range(c.nV)]
    OT = dscr("ot", [4 * c.BW, TOK], BF16); OTb = [Buf(f"ot{i}", True) for i in range(4 * c.BW // 128)]
    MG = dscr("mg", [D, TOK], BF16)
    X = [xT] + [dscr(f"x{i}", [D, TOK], F32) for i in range(1, 4 * L)] + [yT]
    Xb = [[Buf(f"X{i}_{st}", True) for st in range(c.NST)] for i in range(4 * L + 1)]
    HG = [dscr(f"hg{l}", [D, TOK], BF16) for l in range(L)]
    HGb = [[Buf(f"hg{l}_{st}", True) for st in range(c.NST)] for l in range(L)]
    RS = [dscr(f"rs{l}", [c.NST * 128, TS], F32) for l in range(L)]
    RSb = [[Buf(f"rs{l}_{st}", True) for st in range(c.NST)] for l in range(L)]

    ident = T(kb, "ident", [128, 128], BF16)
    ones = {}
    for d in (c.D, c.QL, c.KVL, 256, 128, 64):
        if d not in ones:
            ones[d] = T(kb, f"ones{d}", [128, 128], BF16)
    onesv = T(kb, "onesv", [128, 128], BF16)
    GT = T(kb, "GT", [128, L * c.NG], F32)
    bm = T(kb, "bm", [128, 2], F32)
    esink = T(kb, "esink", [128, L * c.H], F32)
    lam = T(kb, "lam", [128, 4 * L], F32)
    epst = T(kb, "epst", [128, 2], F32)
    xs = Ring(kb, "xs", 3, [128, TS], F32)
    sqr = Ring(kb, "sq", 2, [128, TS], BF16)
    rstd = T(kb, "rstd", [128, TS], F32)
    sdt = T(kb, "sdt", [128, TS], F32)
    ost = Ring(kb, "ost", 3, [128, TS], F32)
    bst = Ring(kb, "bst", 4, [128, TS], BF16)
    KTm = T(kb, "KTm", [128, c.MEMH * 512], BF16)
    rs2 = T(kb, "rs2", [128, TS], F32)
    CVN = 2048
    cstg = Ring(kb, "cstg", 2, [128, CVN], F32)
    ccv = Ring(kb, "ccv", 2, [128, CVN], BF16)
    Vm = T(kb, "Vm", [128, c.MEMH * 512], BF16)
    PS = [T(kb, f"ps{i}", [128, 512], F32, psum=True) for i in range(7)]
    psb_h = {}
    psrot = [0]

    def ybank():
        t = PS[psrot[0] % 4]
        psrot[0] += 1
        return t

    def mset(t, v):
        kb.op("dve", lambda e: e.memset(t[:], v), writes=[t.b])
    for d, t in ones.items():
        mset(t, 1.0 / d)
    mset(onesv, 1.0)
    kb.op("dve", lambda e: e.memset(epst[:, 0:1], EPS), writes=[epst.b])
    kb.op("dve", lambda e: e.memset(epst[:, 1:2], 3 * EPS), writes=[epst.b])
    epsc = {1: epst[:, 0:1], 3: epst[:, 1:2]}
    kb.dma("sp", GT[:], GT_d[:, :], [], [GT.b], GT.b)
    kb.dma("sp", bm[:], bm_d[:, :], [], [bm.b], bm.b)
    zcol = bm[:, 0:1]
    mcol = bm[:, 1:2]

    s0 = Scope(kb)
    identf = s0.T("identf", [128, 128], F32)
    kb.dma("sp", identf[:], ident_d[:, :], [], [identf.b], identf.b)
    kb.op("dve", lambda e: e.tensor_copy(out=ident[:], in_=identf[:]), reads=[identf.b], writes=[ident.b])
    onesf = s0.T("onesf", [128, 128], F32)
    mset(onesf, 1.0)
    for l in range(L):
        g0 = l * c.NG
        kb.op("act", lambda e: e.activation(out=esink[:, l * c.H:(l + 1) * c.H],
                                            in_=GT[:, g0 + c.g_sink:g0 + c.g_sink + c.H], func=AF.Exp),
              reads=[GT.b], writes=[esink.b])
        lt = s0.T(f"lamtmp{l}", [128, 4], F32)
        for j in range(2):
            kb.op("dve", lambda e: e.tensor_tensor(out=lt[:, j:j + 1], in0=GT[:, g0 + c.g_lam + 2 * j:g0 + c.g_lam + 2 * j + 1],
                                                   in1=GT[:, g0 + c.g_lam + 2 * j + 1:g0 + c.g_lam + 2 * j + 2], op=ALU.mult),
                  reads=[GT.b], writes=[lt.b])
        pl = PS[6]
        kb.op("pe", lambda e: e.matmul(pl[:, 0:2], onesf[:], lt[:, 0:2], start=True, stop=True),
              reads=[onesf.b, lt.b], writes=[pl.b])
        kb.op("act", lambda e: e.activation(out=lt[:, 2:4], in_=pl[:, 0:2], func=AF.Exp), reads=[pl.b], writes=[lt.b])
        lam_init = 0.8 - 0.6 * math.exp(-0.3 * l)
        kb.op("dve", lambda e: e.scalar_tensor_tensor(out=lam[:, 4 * l:4 * l + 1], in0=lt[:, 2:3], scalar=lam_init,
                                                      in1=lt[:, 3:4], op0=ALU.add, op1=ALU.subtract),
              reads=[lt.b], writes=[lam.b])
        kb.op("dve", lambda e: e.tensor_scalar(out=lam[:, 4 * l + 1:4 * l + 2], in0=lam[:, 4 * l:4 * l + 1],
                                               scalar1=-1.0, scalar2=None, op0=ALU.mult),
              reads=[lam.b], writes=[lam.b])

    order0 = ["wa", "wqu", "wkvu"]
    order1 = ["mkv", "wg", "wb", "wo", "mq", "mo", "fi", "fo"]
    units = []
    done_at = {}
    for l in range(L):
        for nm in order0 + order1:
            s = specs[l][nm]
            for j, blk in enumerate(s.blocks):
                kc = s.kc
                segs = blk
                if segs[-1][0] == "__nk__":
                    kc = segs[-1][1]; segs = segs[:-1]
                bw = sum(sg[4] for sg in segs)
                step = max(1, CVN // bw)
                for k0 in range(0, kc, step):
                    units.append((l, s, j, k0, min(step, kc - k0), segs, bw))
            done_at[(l, nm)] = len(units)
    cst = {"ld": 0, "cv": 0, "sync": True, "pend": []}

    def conv_load():
        u = cst["ld"]
        l, s, j, k0, nk, segs, bw = units[u]
        st_ = cstg.next()
        off = 0
        sview = st_[:, 0:nk * bw].rearrange("p (k c) -> p k c", k=nk)
        q = "sp" if cst["sync"] else "act"
        for (srcn, sub, row0, col0, w) in segs:
            wd = W[srcn]
            r0 = row0 + k0 * 128
            if sub is None:
                a = wd[l, r0:r0 + nk * 128, col0:col0 + w]
            else:
                a = wd[l, sub, r0:r0 + nk * 128, col0:col0 + w]
            a = a.rearrange("(k p) c -> p k c", p=128)
            kb.dma(q, sview[:, :, off:off + w], a, [], [st_.b], st_.b)
            off += w
        cst["pend"].append((u, st_))
        cst["ld"] += 1

    def conv_cast():
        u, st_ = cst["pend"].pop(0)
        l, s, j, k0, nk, segs, bw = units[u]
        cv_ = ccv.next()
        n_ = nk * bw
        ce = ["dve", "act", "pool"][u % 3] if cst["sync"] else "act"
        if ce == "act":
            kb.op("act", lambda e: e.activation(out=cv_[:, 0:n_], in_=st_[:, 0:n_], func=AF.Copy),
                  reads=[st_.b], writes=[cv_.b])
        else:
            kb.op(ce, lambda e: e.tensor_copy(out=cv_[:, 0:n_], in_=st_[:, 0:n_]), reads=[st_.b], writes=[cv_.b])
        kb.dma("pool" if cst["sync"] else "act", s.dram[j * 128:(j + 1) * 128, k0 * bw:k0 * bw + n_], cv_[:, 0:n_],
               [cv_.b], [s.bufs[j]], cv_.b)
        cst["cv"] += 1

    def conv_step():
        if cst["cv"] >= len(units):
            return
        if len(cst["pend"]) >= 2 or cst["ld"] >= len(units):
            conv_cast()
        if cst["ld"] < len(units):
            conv_load()

    def pump(n=1):
        for _ in range(n):
            conv_step()

    def ensure(l, names):
        tgt = max(done_at[(l, nm)] for nm in names)
        while cst["cv"] < tgt:
            conv_step()

    ensure(0, order0)
    cst["sync"] = False
    s0.close()

    cur = {}

    def wload(spec, j, q="sp"):
        slot = cur["wring"].next()
        kc = spec.kc
        blk = spec.blocks[j]
        if blk[-1][0] == "__nk__":
            kc = blk[-1][1]
            bw = sum(sg[4] for sg in blk[:-1])
        else:
            bw = sum(sg[4] for sg in blk)
        nn = kc * bw
        assert nn <= WSLOT, (spec.name, nn)
        kb.dma(q, slot[:, 0:nn], spec.dram[j * 128:(j + 1) * 128, 0:nn], [spec.bufs[j]], [slot.b], slot.b)
        pump(cur.get("rate", 1))
        return slot, bw, kc

    def mm_group(ps_ap, psb, pairs, reads):
        def fn(e):
            ins = None
            nn = len(pairs)
            for i, (a, b) in enumerate(pairs):
                ins = e.matmul(ps_ap, a, b, start=(i == 0), stop=(i == nn - 1))
            return ins
        kb.op("pe", fn, reads=reads, writes=[psb])

    def rstd_from(ss_ps, eps_eff, ntok=TS):
        kb.op("act", lambda e: e.activation(out=sdt[:, 0:ntok], in_=ss_ps[:, 0:ntok], func=AF.Sqrt, bias=epsc[eps_eff], scale=1.0),
              reads=[ss_ps.b, epst.b], writes=[sdt.b])
        kb.op("dve", lambda e: e.reciprocal(out=rstd[:, 0:ntok], in_=sdt[:, 0:ntok]), reads=[sdt.b], writes=[rstd.b])

    def norm_stream(xsrc, xbuf, t0, ntok, gcol0):
        hT = cur["hT"]
        onesD = ones[c.D]
        ssp = PS[4]
        for cc in range(NC):
            xt = xs.next()
            kb.dma("pool", xt[:, 0:ntok], xsrc[cc * 128:(cc + 1) * 128, t0:t0 + ntok], [xbuf], [xt.b], xt.b)
            sq = sqr.next()
            kb.op("act", lambda e: e.activation(out=sq[:, 0:ntok], in_=xt[:, 0:ntok], func=AF.Square),
                  reads=[xt.b], writes=[sq.b])
            kb.op("pe", lambda e: e.matmul(ssp[:, 0:ntok], onesD[:], sq[:, 0:ntok], start=(cc == 0), stop=(cc == NC - 1)),
                  reads=[onesD.b, sq.b], writes=[ssp.b])
        rstd_from(ssp, 1, ntok)
        for cc in range(NC):
            xt = xs.next()
            kb.dma("pool", xt[:, 0:ntok], xsrc[cc * 128:(cc + 1) * 128, t0:t0 + ntok], [xbuf], [xt.b], xt.b)
            kb.op("dve", lambda e: e.scalar_tensor_tensor(out=hT[:, cc * TS:cc * TS + ntok], in0=xt[:, 0:ntok],
                                                          scalar=GT[:, gcol0 + cc:gcol0 + cc + 1], in1=rstd[:, 0:ntok],
                                                          op0=ALU.mult, op1=ALU.mult),
                  reads=[xt.b, GT.b, rstd.b], writes=[hT.b])

    def hsave(l, st):
        hT = cur["hT"]; t0 = st * TS
        kb.dma("pool", HG[l][:, t0:t0 + TS].rearrange("(b p) t -> p b t", p=128),
               hT[:, 0:NC * TS].rearrange("p (b t) -> p b t", b=NC), [hT.b], [HGb[l][st]], hT.b)

    def hscale(src_t):
        hT = cur["hT"]
        for cc in range(NC):
            eng = "dve" if cc % 4 != 3 else "pool"
            kb.op(eng, lambda e: e.tensor_tensor(out=hT[:, cc * TS:(cc + 1) * TS], in0=hT[:, cc * TS:(cc + 1) * TS],
                                                 in1=src_t[:, :], op=ALU.mult),
                  reads=[hT.b, src_t.b], writes=[hT.b])

    def hload(l, st, scaled):
        hT = cur["hT"]; t0 = st * TS
        kb.dma("sp", hT[:, 0:NC * TS].rearrange("p (b t) -> p b t", b=NC),
               HG[l][:, t0:t0 + TS].rearrange("(b p) t -> p b t", p=128), [HGb[l][st]], [hT.b], hT.b)
        if scaled:
            kb.dma("sp", rs2[:, :], RS[l][st * 128:(st + 1) * 128, :], [RSb[l][st]], [rs2.b], rs2.b)
            hscale(rs2)

    def dense(slot, bw, kc, col0, ncols, rhs_t, rhs_stride, ntok, ps, rhs_off=0):
        pairs = [(slot[:, k * bw + col0:k * bw + col0 + ncols],
                  rhs_t[:, rhs_off + k * rhs_stride:rhs_off + k * rhs_stride + ntok]) for k in range(kc)]
        mm_group(ps[0:ncols, 0:ntok], ps.b, pairs, [slot.b, rhs_t.b])

    def epi_hn(ps, gcol, qkid, t0, ntok=TS, dst_sb=None, dst_off=0):
        sq = sqr.next()
        kb.op("act", lambda e: e.activation(out=sq[:, 0:ntok], in_=ps[:, 0:ntok], func=AF.Square), reads=[ps.b], writes=[sq.b])
        ssp = PS[5]
        kb.op("pe", lambda e: e.matmul(ssp[:, 0:ntok], ones[128][:], sq[:, 0:ntok], start=True, stop=True),
              reads=[ones[128].b, sq.b], writes=[ssp.b])
        rstd_from(ssp, 1, ntok)
        if dst_sb is None:
            o = bst.next()
            kb.op("dve", lambda e: e.scalar_tensor_tensor(out=o[:, 0:ntok], in0=ps[:, 0:ntok], scalar=GT[:, gcol:gcol + 1],
                                                          in1=rstd[:, 0:ntok], op0=ALU.mult, op1=ALU.mult),
                  reads=[ps.b, GT.b, rstd.b], writes=[o.b])
            kb.dma("pool", QK[qkid * 128:(qkid + 1) * 128, t0:t0 + ntok], o[:, 0:ntok], [o.b], [QKb[qkid]], o.b)
        else:
            kb.op("dve", lambda e: e.scalar_tensor_tensor(out=dst_sb[:, dst_off:dst_off + ntok], in0=ps[:, 0:ntok],
                                                          scalar=GT[:, gcol:gcol + 1], in1=rstd[:, 0:ntok],
                                                          op0=ALU.mult, op1=ALU.mult),
                  reads=[ps.b, GT.b, rstd.b], writes=[dst_sb.b])

    def epi_v(ps, vid, t0, ntok=TS, dst_sb=None, dst_off=0):
        vb = bst.next()
        PSB = psb_h["t"]
        kb.op("act", lambda e: e.activation(out=vb[:, 0:ntok], in_=ps[:, 0:ntok], func=AF.Copy), reads=[ps.b], writes=[vb.b])
        nch = ntok // 128
        def fn(e):
            ins = None
            for j in range(nch):
                ins = e.transpose(PSB[:, j * 128:(j + 1) * 128], vb[:, j * 128:(j + 1) * 128], ident[:])
            return ins
        kb.op("pe", fn, reads=[vb.b, ident.b], writes=[PSB.b])
        if dst_sb is None:
            o = bst.next()
            kb.op("dve", lambda e: e.tensor_copy(out=o[:, 0:ntok], in_=PSB[:, 0:ntok]), reads=[PSB.b], writes=[o.b])
            ch0 = t0 // 128
            kb.dma("pool", VS[vid * 128:(vid + 1) * 128, ch0 * 128:ch0 * 128 + ntok], o[:, 0:ntok], [o.b], [VSb[vid]], o.b)
        else:
            kb.op("dve", lambda e: e.tensor_copy(out=dst_sb[:, dst_off:dst_off + ntok], in_=PSB[:, 0:ntok]),
                  reads=[PSB.b], writes=[dst_sb.b])

    def attn_qtile(q_parts, k_parts, kcs, v_of, nv, scale, o_ps, den_ps, ptr, zr, st_ps, nq=TS, coef=None):
        n = len(kcs)
        pts = [None] * n

        def qk(i):
            kc, bias, bbuf, bcol, cf = kcs[i]
            cf = coef if cf is None else cf
            st = st_ps[i % len(st_ps)]
            kp = k_parts(kc)
            pairs = []
            reads = []
            for (kt, koff, npart), (qt_, qoff, _) in zip(kp, q_parts):
                pairs.append((kt[0:npart, koff:koff + 128], qt_[0:npart, qoff:qoff + nq]))
                reads += [kt.b, qt_.b]
            mm_group(st[:, 0:nq], st.b, pairs, reads)
            src = st
            if bias is not None:
                z = zr.next()
                kb.op("dve", lambda e: e.scalar_tensor_tensor(out=z[:, 0:nq], in0=bias, scalar=cf, in1=st[:, 0:nq],
                                                              op0=ALU.mult, op1=ALU.add),
                      reads=[st.b, bbuf], writes=[z.b])
                src = z
            pt = ptr.next()
            kb.op("act", lambda e: e.activation(out=pt[:, 0:nq], in_=src[:, 0:nq], func=AF.Exp, bias=bcol, scale=scale),
                  reads=[src.b, bm.b], writes=[pt.b])
            pts[i] = pt

        def pv(i):
            kc = kcs[i][0]
            pt = pts[i]
            for v in range(nv):
                vt, vap = v_of(kc, v)
                kb.op("pe", lambda e: e.matmul(o_ps[v][:, 0:nq], vap, pt[:, 0:nq], start=(i == 0), stop=(i == n - 1)),
                      reads=[vt.b, pt.b], writes=[o_ps[v].b])
            kb.op("pe", lambda e: e.matmul(den_ps[:, 0:nq], onesv[:], pt[:, 0:nq], start=(i == 0), stop=(i == n - 1)),
                  reads=[onesv.b, pt.b], writes=[den_ps.b])

        LA = min(3, len(st_ps) - 1) if len(st_ps) > 1 else 0
        for i in range(min(LA, n)):
            qk(i)
        for i in range(n):
            if i + LA < n:
                qk(i + LA)
            pv(i)
        pump(cur.get("arate", 0))

    sl_dil = alibi(c.H); sl_win = alibi(c.H); sl_dif = alibi(c.DH)
    SC128 = 128.0 ** -0.5
    SCMLA = 3.0 / math.sqrt(192.0)
    NQT = TOK // TS
    hb = c.HALF // 128

    for l in range(L):
        sp = specs[l]
        g0 = l * c.NG
        xin, xinb = X[4 * l], Xb[4 * l]
        x1, x1b = X[4 * l + 1], Xb[4 * l + 1]
        x2, x2b = X[4 * l + 2], Xb[4 * l + 2]
        x2h, x2hb = X[4 * l + 3], Xb[4 * l + 3]
        x3, x3b = X[4 * l + 4], Xb[4 * l + 4]

        if l > 0:
            ensure(l, order0)
        sA = Scope(kb)
        cur["rate"] = 1
        psb_h["t"] = sA.T("psb", [128, 1024], BF16, psum=True)
        cur["hT"] = sA.T("hT", [128, NC * TS], BF16)
        cur["wring"] = sA.ring("wr", 3, [128, WSLOT], BF16)
        hT = cur["hT"]
        nql = c.QL // 128; nkl = c.KVL // 128
        qlat_f = sA.T("qlat_f", [128, nql * TS], F32)
        qlatN = sA.T("qlatN", [128, nql * TS], BF16)
        kvlat_f = sA.T("kvlat_f", [128, nkl * TS], F32)
        kvN = sA.T("kvN", [128, nkl * TS], BF16)
        kpe = sA.T("kpe", [64, TS], F32)
        sqkpe = sA.T("sqkpe", [64, TS], BF16)
        qr = sA.T("qr", [64, TS], F32)
        rt1 = sA.T("rt1", [64, TS], F32)
        rt2 = sA.T("rt2", [64, TS], F32)
        cosS = sA.T("cosS", [64, TS], F32)
        sinS = sA.T("sinS", [64, TS], F32)
        sqn = sA.T("sqn", [128, TS], BF16)
        sqrr = sA.T("sqrr", [64, TS], BF16)

        def rope(ps_r, ps_s, dst):
            kb.op("dve", lambda e: e.tensor_tensor(out=rt1[:, :], in0=ps_r[0:64, :], in1=cosS[:, :], op=ALU.mult),
                  reads=[ps_r.b, cosS.b], writes=[rt1.b])
            kb.op("dve", lambda e: e.tensor_tensor(out=rt2[:, :], in0=ps_s[0:64, :], in1=sinS[:, :], op=ALU.mult),
                  reads=[ps_s.b, sinS.b], writes=[rt2.b])
            kb.op("pool", lambda e: e.tensor_tensor(out=dst[:, :], in0=rt1[:, :], in1=rt2[:, :], op=ALU.add),
                  reads=[rt1.b, rt2.b], writes=[dst.b])

        cosb = Buf("cosd"); sinb = Buf("sind")
        for st in range(c.NST):
            t0 = st * TS
            if l == 0:
                norm_stream(xin, xinb[st], t0, TS, g0 + c.g_lnmix)
                hsave(0, st)
            else:
                hload(l, st, True)
            kb.dma("pool", cosS[:, :], cos_d[:, t0:t0 + TS], [cosb], [cosS.b], cosS.b)
            kb.dma("pool", sinS[:, :], sin_d[:, t0:t0 + TS], [sinb], [sinS.b], sinS.b)
            wa = sp["wa"]
            for j, blk in enumerate(wa.blocks):
                slot, bw, kc = wload(wa, j)
                for si, kind in enumerate(wa.kinds[j]):
                    if kind[0] == "kpe":
                        pr, psn = PS[5], PS[6]
                        dense(slot, bw, kc, 0, 64, hT, TS, TS, pr)
                        dense(slot, bw, kc, 64, 64, hT, TS, TS, psn)
                        rope(pr, psn, kpe)
                        kb.op("act", lambda e: e.activation(out=sqkpe[:, :], in_=kpe[:, :], func=AF.Square),
                              reads=[kpe.b], writes=[sqkpe.b])
                        continue
                    ps = ybank()
                    dense(slot, bw, kc, si * 128, 128, hT, TS, TS, ps)
                    if kind[0] in ("latq", "latkv"):
                        i = kind[1]
                        lf, nl, od = (qlat_f, nql, ones[c.QL]) if kind[0] == "latq" else (kvlat_f, nkl, ones[c.KVL])
                        kb.op("act", lambda e: e.activation(out=lf[:, i * TS:(i + 1) * TS], in_=ps[:, :], func=AF.Copy),
                              reads=[ps.b], writes=[lf.b])
                        sq = sqr.next()
                        kb.op("act", lambda e: e.activation(out=sq[:, :], in_=ps[:, :], func=AF.Square), reads=[ps.b], writes=[sq.b])
                        ssl = PS[6]
                        kb.op("pe", lambda e: e.matmul(ssl[:, :], od[:], sq[:, :], start=(i == 0), stop=(i == nl - 1)),
                              reads=[od.b, sq.b], writes=[ssl.b])
                        if i == nl - 1:
                            rstd_from(ssl, 1)
                            gq = g0 + (c.g_qa if kind[0] == "latq" else c.g_kva)
                            dstN = qlatN if kind[0] == "latq" else kvN
                            for ii in range(nl):
                                kb.op("dve", lambda e: e.scalar_tensor_tensor(out=dstN[:, ii * TS:(ii + 1) * TS], in0=lf[:, ii * TS:(ii + 1) * TS],
                                                                              scalar=GT[:, gq + ii:gq + ii + 1], in1=rstd[:, :],
                                                                              op0=ALU.mult, op1=ALU.mult),
                                      reads=[lf.b, GT.b, rstd.b], writes=[dstN.b])
                    elif kind[0] == "hn":
                        epi_hn(ps, g0 + kind[2], kind[1], t0)
                    elif kind[0] == "v":
                        epi_v(ps, kind[1], t0)
            for h in range(c.H):
                slot, bw, kc = wload(sp["wqu"], h)
                pn = ybank(); pr, psn = PS[5], PS[6]
                dense(slot, bw, kc, 0, 128, qlatN, TS, TS, pn)
                dense(slot, bw, kc, 128, 64, qlatN, TS, TS, pr)
                dense(slot, bw, kc, 192, 64, qlatN, TS, TS, psn)
                rope(pr, psn, qr)
                kb.op("act", lambda e: e.activation(out=sqn[:, :], in_=pn[:, :], func=AF.Square), reads=[pn.b], writes=[sqn.b])
                kb.op("act", lambda e: e.activation(out=sqrr[:, :], in_=qr[:, :], func=AF.Square), reads=[qr.b], writes=[sqrr.b])
                ssp = PS[4]
                mm_group(ssp[:, :], ssp.b, [(ones[64][:], sqn[:, :]), (ones[64][0:64, :], sqrr[:, :])], [ones[64].b, sqn.b, sqrr.b])
                rstd_from(ssp, 3)
                o = bst.next()
                kb.op("dve", lambda e: e.scalar_tensor_tensor(out=o[:, :], in0=pn[:, :], scalar=GT[:, g0 + c.g_mla:g0 + c.g_mla + 1],
                                                              in1=rstd[:, :], op0=ALU.mult, op1=ALU.mult),
                      reads=[pn.b, GT.b, rstd.b], writes=[o.b])
                qid = c.q_mla_n + h
                kb.dma("pool", QK[qid * 128:(qid + 1) * 128, t0:t0 + TS], o[:, :], [o.b], [QKb[qid]], o.b)
                o2 = bst.next()
                kb.op("dve", lambda e: e.scalar_tensor_tensor(out=o2[0:64, :], in0=qr[:, :], scalar=GT[0:64, g0 + c.g_mla + 1:g0 + c.g_mla + 2],
                                                              in1=rstd[0:64, :], op0=ALU.mult, op1=ALU.mult),
                      reads=[qr.b, GT.b, rstd.b], writes=[o2.b])
                qid = c.q_mla_r + h
                kb.dma("pool", QK[qid * 128:qid * 128 + 64, t0:t0 + TS], o2[0:64, :], [o2.b], [QKb[qid]], o2.b)
            for h in range(c.H):
                slot, bw, kc = wload(sp["wkvu"], h)
                pn = ybank(); pv_ = ybank()
                dense(slot, bw, kc, 0, 128, kvN, TS, TS, pn)
                dense(slot, bw, kc, 128, 128, kvN, TS, TS, pv_)
                epi_v(pv_, c.v_mla + h, t0)
                kb.op("act", lambda e: e.activation(out=sqn[:, :], in_=pn[:, :], func=AF.Square), reads=[pn.b], writes=[sqn.b])
                ssp = PS[4]
                mm_group(ssp[:, :], ssp.b, [(ones[64][:], sqn[:, :]), (ones[64][0:64, :], sqkpe[:, :])], [ones[64].b, sqn.b, sqkpe.b])
                rstd_from(ssp, 3)
                o = bst.next()
                kb.op("dve", lambda e: e.scalar_tensor_tensor(out=o[:, :], in0=pn[:, :], scalar=GT[:, g0 + c.g_mla + 2:g0 + c.g_mla + 3],
                                                              in1=rstd[:, :], op0=ALU.mult, op1=ALU.mult),
                      reads=[pn.b, GT.b, rstd.b], writes=[o.b])
                qid = c.k_mla_n + h
                kb.dma("pool", QK[qid * 128:(qid + 1) * 128, t0:t0 + TS], o[:, :], [o.b], [QKb[qid]], o.b)
                o2 = bst.next()
                kb.op("dve", lambda e: e.scalar_tensor_tensor(out=o2[0:64, :], in0=kpe[:, :], scalar=GT[0:64, g0 + c.g_mla + 3:g0 + c.g_mla + 4],
                                                              in1=rstd[0:64, :], op0=ALU.mult, op1=ALU.mult),
                      reads=[kpe.b, GT.b, rstd.b], writes=[o2.b])
                qid = c.k_mla_r + h
                kb.dma("pool", QK[qid * 128:qid * 128 + 64, t0:t0 + TS], o2[0:64, :], [o2.b], [QKb[qid]], o2.b)
        sA.close()

        ST_PS = [PS[0], PS[1], PS[6]]

        def attn_scope(names2):
            sB = Scope(kb)
            NH = 2
            tiles = {nm: [sB.T(f"{nm}{i}", [128, TOK], BF16) for i in range(NH)] for nm in names2}
            ptr = sB.ring("pt", 5, [128, TS], BF16)
            zr = sB.ring("z", 5, [128, TS], F32)
            rec = sB.T("rec", [128, TS], F32)
            ps7 = sB.T("ps7", [128, 512], F32, psum=True)
            ST_PS[:] = [PS[0], PS[1], PS[6], ps7]
            return sB, tiles, ptr, zr, rec

        hcnt = [0]
        def load_head(tiles, ids):
            i = hcnt[0] % 2; hcnt[0] += 1
            out = {}
            for nm, (kind, sid, rows) in ids.items():
                tl = tiles[nm][i]
                if kind == "qk":
                    kb.dma("sp", tl[0:rows, :], QK[sid * 128:sid * 128 + rows, :], [QKb[sid]], [tl.b], tl.b)
                else:
                    kb.dma("sp", tl[:, :], VS[sid * 128:(sid + 1) * 128, :], [VSb[sid]], [tl.b], tl.b)
                out[nm] = tl
            return out

        def finalize_store(rec, o_ps_list, den_ps, chan_blocks, t0, sinkcol=None):
            if sinkcol is not None:
                kb.op("dve", lambda e: e.tensor_scalar(out=rec[:, :], in0=den_ps[:, :], scalar1=sinkcol, scalar2=None, op0=ALU.add),
                      reads=[den_ps.b, esink.b], writes=[rec.b])
                kb.op("dve", lambda e: e.reciprocal(out=rec[:, :], in_=rec[:, :]), reads=[rec.b], writes=[rec.b])
            else:
                kb.op("dve", lambda e: e.reciprocal(out=rec[:, :], in_=den_ps[:, :]), reads=[den_ps.b], writes=[rec.b])
            for ops, cb in zip(o_ps_list, chan_blocks):
                o = bst.next()
                kb.op("dve", lambda e: e.tensor_tensor(out=o[:, :], in0=ops[:, :], in1=rec[:, :], op=ALU.mult),
                      reads=[ops.b, rec.b], writes=[o.b])
                kb.dma("pool", OT[cb * 128:(cb + 1) * 128, t0:t0 + TS], o[:, :], [o.b], [OTb[cb]], o.b)

        cur["arate"] = 2
        OD = [(PS[2], PS[3]), (PS[4], PS[5])]

        sB, tl, ptr, zr, rec = attn_scope(["qn", "kn", "q2", "k2", "v"])
        for h in range(c.H):
            hd = load_head(tl, {"qn": ("qk", c.q_mla_n + h, 128), "q2": ("qk", c.q_mla_r + h, 64),
                                "kn": ("qk", c.k_mla_n + h, 128), "k2": ("qk", c.k_mla_r + h, 64),
                                "v": ("v", c.v_mla + h, 128)})
            for qt in range(NQT):
                qh = (qt * TS) // c.HALF
                ops_, dps_ = OD[qt % 2]
                kcs = [(kc, None, None, (zcol if (kc // hb) == qh else mcol), None) for kc in range(c.NCH)]
                attn_qtile([(hd["qn"], qt * TS, 128), (hd["q2"], qt * TS, 64)],
                           lambda kc: [(hd["kn"], kc * 128, 128), (hd["k2"], kc * 128, 64)],
                           kcs, lambda kc, v: (hd["v"], hd["v"][:, kc * 128:(kc + 1) * 128]), 1, SCMLA,
                           [ops_], dps_, ptr, zr, ST_PS)
                finalize_store(rec, [ops_], dps_, [0 * c.H + h], qt * TS)
        sB.close()

        sB, tl, ptr, zr, rec = attn_scope(["qn", "kn", "v"])
        tdl = sB.T("tdl", [128, WD], F32); tlm = sB.T("tlm", [128, WL], F32)
        tb1 = Buf("tabd"); tb2 = Buf("tabl")
        kb.dma("sp", tdl[:, :], tabd_d[:, :], [tb1], [tdl.b], tdl.b)
        kb.dma("sp", tlm[:, :], tabl_d[:, :], [tb2], [tlm.b], tlm.b)
        tcomb = [sB.T(f"tcomb{i}", [128, WL], F32) for i in range(2)]
        for h in range(c.H):
            hd = load_head(tl, {"qn": ("qk", c.q_dil + h, 128), "kn": ("qk", c.k_dil + h, 128), "v": ("v", c.v_dil + h, 128)})
            coef = sl_dil[h] / SC128
            tcb = tcomb[h % 2]
            kb.op("dve", lambda e: e.scalar_tensor_tensor(out=tcb[:, :], in0=tdl[:, 768:768 + WL], scalar=coef, in1=tlm[:, :],
                                                          op0=ALU.mult, op1=ALU.add),
                  reads=[tdl.b, tlm.b], writes=[tcb.b])
            for qt in range(NQT):
                qh = (qt * TS) // c.HALF
                ops_, dps_ = OD[qt % 2]
                kcs = []
                for rel in range(-8, 12):
                    kc = qt * 4 + rel
                    if kc < 0 or kc >= c.NCH:
                        continue
                    bc_ = (zcol if (kc // hb) == qh else mcol)
                    if -2 <= rel <= 5:
                        kcs.append((kc, tcb[:, (5 - rel) * 128:(5 - rel) * 128 + TS], tcb.b, bc_, 1.0))
                    else:
                        kcs.append((kc, tdl[:, (11 - rel) * 128:(11 - rel) * 128 + TS], tdl.b, bc_, None))
                attn_qtile([(hd["qn"], qt * TS, 128)], lambda kc: [(hd["kn"], kc * 128, 128)],
                           kcs, lambda kc, v: (hd["v"], hd["v"][:, kc * 128:(kc + 1) * 128]), 1, SC128,
                           [ops_], dps_, ptr, zr, ST_PS, coef=coef)
                finalize_store(rec, [ops_], dps_, [1 * c.H + h], qt * TS)
        sB.close()

        sB, tl, ptr, zr, rec = attn_scope(["qn", "kn", "v"])
        twn = sB.T("twn", [128, WW], F32)
        tb3 = Buf("tabw")
        kb.dma("sp", twn[:, :], tabw_d[:, :], [tb3], [twn.b], twn.b)
        for h in range(c.H):
            g = h // (c.H // c.WKV)
            hd = load_head(tl, {"qn": ("qk", c.q_win + h, 128), "kn": ("qk", c.k_win + g, 128), "v": ("v", c.v_win + g, 128)})
            coef = sl_win[h] / SC128
            for qt in range(NQT):
                qh = (qt * TS) // c.HALF
                ops_, dps_ = OD[qt % 2]
                kcs = []
                for rel in range(-1, 5):
                    kc = qt * 4 + rel
                    if kc < 0 or kc >= c.NCH:
                        continue
                    kcs.append((kc, twn[:, (4 - rel) * 128:(4 - rel) * 128 + TS], twn.b, (zcol if (kc // hb) == qh else mcol), None))
                attn_qtile([(hd["qn"], qt * TS, 128)], lambda kc: [(hd["kn"], kc * 128, 128)],
                           kcs, lambda kc, v: (hd["v"], hd["v"][:, kc * 128:(kc + 1) * 128]), 1, SC128,
                           [ops_], dps_, ptr, zr, ST_PS, coef=coef)
                finalize_store(rec, [ops_], dps_, [2 * c.H + h], qt * TS, sinkcol=esink[:, l * c.H + h:l * c.H + h + 1])
        sB.close()

        sB, tl, ptr, zr, rec = attn_scope(["qn", "kn", "q2", "k2", "v", "v2"])
        tab_a = sB.T("taba", [128, WA], mybir.dt.int16)
        tb4 = Buf("taba_d")
        kb.dma("sp", tab_a[:, :], taba_d[:, :], [tb4], [tab_a.b], tab_a.b)
        om = [sB.T(f"om{i}", [128, TS], F32) for i in range(4)]
        od_ = [sB.T(f"od{i}", [128, TS], F32) for i in range(2)]
        sqo = [sB.T(f"sqo{i}", [128, TS], BF16) for i in range(2)]
        lam_init = 0.8 - 0.6 * math.exp(-0.3 * l)
        relmax = c.NCH - 1
        for h in range(c.DH):
            hd = load_head(tl, {"qn": ("qk", c.q_dif + 2 * h, 128), "q2": ("qk", c.q_dif + 2 * h + 1, 128),
                                "kn": ("qk", c.k_dif + 2 * h, 128), "k2": ("qk", c.k_dif + 2 * h + 1, 128),
                                "v": ("v", c.v_dif + 2 * h, 128), "v2": ("v", c.v_dif + 2 * h + 1, 128)})
            coef = -sl_dif[h] / SC128
            for qt in range(NQT):
                qh = (qt * TS) // c.HALF
                for m in range(2):
                    qtile = hd["qn"] if m == 0 else hd["q2"]
                    ktile = hd["kn"] if m == 0 else hd["k2"]
                    kcs = []
                    for kc in range(c.NCH):
                        rel = kc - qt * 4
                        u0 = (relmax - rel) * 128
                        kcs.append((kc, tab_a[:, u0:u0 + TS], tab_a.b, (zcol if (kc // hb) == qh else mcol), None))
                    attn_qtile([(qtile, qt * TS, 128)], lambda kc, ktile=ktile: [(ktile, kc * 128, 128)],
                               kcs, lambda kc, v: ((hd["v"], hd["v"][:, kc * 128:(kc + 1) * 128]) if v == 0 else
                                                   (hd["v2"], hd["v2"][:, kc * 128:(kc + 1) * 128])), 2, SC128,
                               [PS[2], PS[3]], PS[4], ptr, zr, ST_PS, coef=coef)
                    kb.op("dve", lambda e: e.reciprocal(out=rec[:, :], in_=PS[4][:, :]), reads=[PS[4].b], writes=[rec.b])
                    for v in range(2):
                        kb.op("dve", lambda e: e.tensor_tensor(out=om[m * 2 + v][:, :], in0=PS[2 + v][:, :], in1=rec[:, :], op=ALU.mult),
                              reads=[PS[2 + v].b, rec.b], writes=[om[m * 2 + v].b])
                for v in range(2):
                    kb.op("dve", lambda e: e.scalar_tensor_tensor(out=od_[v][:, :], in0=om[2 + v][:, :], scalar=lam[:, 4 * l + 1:4 * l + 2],
                                                                  in1=om[v][:, :], op0=ALU.mult, op1=ALU.add),
                          reads=[om[2 + v].b, om[v].b, lam.b], writes=[od_[v].b])
                    kb.op("act", lambda e: e.activation(out=sqo[v][:, :], in_=od_[v][:, :], func=AF.Square), reads=[od_[v].b], writes=[sqo[v].b])
                ssp = PS[5]
                mm_group(ssp[:, :], ssp.b, [(ones[256][:], sqo[0][:, :]), (ones[256][:], sqo[1][:, :])], [ones[256].b, sqo[0].b, sqo[1].b])
                rstd_from(ssp, 1)
                for v in range(2):
                    o = bst.next()
                    gc = g0 + c.g_subln + v
                    kb.op("dve", lambda e: e.scalar_tensor_tensor(out=od_[v][:, :], in0=od_[v][:, :], scalar=GT[:, gc:gc + 1],
                                                                  in1=rstd[:, :], op0=ALU.mult, op1=ALU.mult),
                          reads=[od_[v].b, GT.b, rstd.b], writes=[od_[v].b])
                    kb.op("act", lambda e: e.activation(out=o[:, :], in_=od_[v][:, :], func=AF.Copy, scale=(1.0 - lam_init)),
                          reads=[od_[v].b], writes=[o.b])
                    cb = 3 * c.H + 2 * h + v
                    kb.dma("pool", OT[cb * 128:(cb + 1) * 128, qt * TS:(qt + 1) * TS], o[:, :], [o.b], [OTb[cb]], o.b)
        sB.close()

        ensure(l, order1)
        sC = Scope(kb)
        cur["arate"] = 0
        psb_h["t"] = sC.T("psb", [128, 1024], BF16, psum=True)
        cur["hT"] = sC.T("hT", [128, NC * TS], BF16)
        cur["wring"] = sC.ring("wr", 3, [128, WSLOT], BF16)
        hT = cur["hT"]
        nblk_o = 4 * c.BW // 128
        bigw = max(nblk_o, c.FCH, NC) * TS
        big = sC.T("big", [128, bigw], BF16)
        macc = sC.T("macc", [128, TS], F32)
        mtmp = sC.T("mtmp", [128, TS], F32)
        sg_r = sC.ring("sg", 2, [128, TS], F32)
        qm = sC.T("qm", [128, c.MEMH * TS], BF16)
        om_ = sC.T("omem", [128, c.MEMH * TS], BF16)
        recm = sC.T("recm", [128, TS], F32)
        ptm = sC.ring("ptm", 2, [128, TS], BF16)
        mgb = [Buf(f"mg{l}_{st}", True) for st in range(c.NST)]

        NM = 2 * c.MEM
        memb = Buf(f"memT{l}")
        norm_stream(memT, memb, 0, NM, g0 + c.g_memln)
        for j in range(len(sp["mkv"].blocks)):
            slot, bw, kc = wload(sp["mkv"], j)
            for si in range(2):
                sb_ = 2 * j + si
                ps = ybank()
                dense(slot, bw, kc, si * 128, 128, hT, TS, NM, ps)
                if sb_ < c.MEMH:
                    epi_hn(ps, g0 + c.g_mem + 1, None, 0, ntok=NM, dst_sb=KTm, dst_off=sb_ * 512)
                else:
                    epi_v(ps, None, 0, ntok=NM, dst_sb=Vm, dst_off=(sb_ - c.MEMH) * 512)

        def residual_out(ps, xsrc, xsb, xdst, xdb, ob, t0, fuse=None):
            xt = xs.next()
            kb.dma("pool", xt[:, :], xsrc[ob * 128:(ob + 1) * 128, t0:t0 + TS], [xsb], [xt.b], xt.b)
            o = ost.next()
            kb.op("dve", lambda e: e.tensor_tensor(out=o[:, :], in0=ps[:, :], in1=xt[:, :], op=ALU.add),
                  reads=[ps.b, xt.b], writes=[o.b])
            kb.dma("pool", xdst[ob * 128:(ob + 1) * 128, t0:t0 + TS], o[:, :], [o.b], [xdb], o.b)
            if fuse is None:
                return
            gcol0, mode, st_ = fuse
            hT_ = cur["hT"]
            sq = sqr.next()
            kb.op("act", lambda e: e.activation(out=sq[:, :], in_=o[:, :], func=AF.Square), reads=[o.b], writes=[sq.b])
            ssn = PS[4]
            kb.op("pe", lambda e: e.matmul(ssn[:, :], ones[c.D][:], sq[:, :], start=(ob == 0), stop=(ob == NC - 1)),
                  reads=[ones[c.D].b, sq.b], writes=[ssn.b])
            gc = GT[:, gcol0 + ob:gcol0 + ob + 1]
            if mode == "hT":
                kb.op("act", lambda e: e.activation(out=hT_[:, ob * TS:(ob + 1) * TS], in_=o[:, :], func=AF.Copy, scale=gc),
                      reads=[o.b, GT.b], writes=[hT_.b])
            else:
                ob_ = bst.next()
                kb.op("act", lambda e: e.activation(out=ob_[:, :], in_=o[:, :], func=AF.Copy, scale=gc),
                      reads=[o.b, GT.b], writes=[ob_.b])
                kb.dma("pool", HG[l + 1][ob * 128:(ob + 1) * 128, t0:t0 + TS], ob_[:, :], [ob_.b], [HGb[l + 1][st_]], ob_.b)
            if ob == NC - 1:
                rstd_from(ssn, 1)
                if mode == "hT":
                    hscale(rstd)
                else:
                    kb.dma("pool", RS[l + 1][st_ * 128:(st_ + 1) * 128, :], rstd[:, :], [rstd.b], [RSb[l + 1][st_]], rstd.b)

        kbw = c.BW // 128
        for st in range(c.NST):
            t0 = st * TS
            hload(l, st, l > 0)
            kb.dma("sp", big[:, 0:nblk_o * TS].rearrange("p (b t) -> p b t", b=nblk_o),
                   OT[:, t0:t0 + TS].rearrange("(b p) t -> p b t", p=128), OTb, [big.b], big.b)
            for ob in range(NC):
                gs = [wload(sp["wg"], 2 * ob + pr) for pr in range(2)]
                bs, bbw, bkc = wload(sp["wb"], ob)
                for i in range(4):
                    gslot, gbw, gkc = gs[i // 2]
                    pg = ybank(); pp = ybank()
                    dense(gslot, gbw, gkc, (i % 2) * 128, 128, hT, TS, TS, pg)
                    dense(bs, bbw, bkc, i * 128, 128, big, TS, TS, pp, rhs_off=i * kbw * TS)
                    sg = sg_r.next()
                    kb.op("act", lambda e: e.activation(out=sg[:, :], in_=pg[:, :], func=AF.Sigmoid), reads=[pg.b], writes=[sg.b])
                    if i == 0:
                        kb.op("dve", lambda e: e.tensor_tensor(out=macc[:, :], in0=pp[:, :], in1=sg[:, :], op=ALU.mult),
                              reads=[pp.b, sg.b], writes=[macc.b])
                    else:
                        kb.op("dve", lambda e: e.tensor_tensor(out=mtmp[:, :], in0=pp[:, :], in1=sg[:, :], op=ALU.mult),
                              reads=[pp.b, sg.b], writes=[mtmp.b])
                        if i < 3:
                            kb.op("pool", lambda e: e.tensor_tensor(out=macc[:, :], in0=macc[:, :], in1=mtmp[:, :], op=ALU.add),
                                  reads=[macc.b, mtmp.b], writes=[macc.b])
                        else:
                            o = bst.next()
                            kb.op("pool", lambda e: e.tensor_tensor(out=o[:, :], in0=macc[:, :], in1=mtmp[:, :], op=ALU.add),
                                  reads=[macc.b, mtmp.b], writes=[o.b])
                            kb.dma("pool", MG[ob * 128:(ob + 1) * 128, t0:t0 + TS], o[:, :], [o.b], [mgb[st]], o.b)
            kb.dma("sp", big[:, 0:NC * TS].rearrange("p (b t) -> p b t", b=NC),
                   MG[:, t0:t0 + TS].rearrange("(b p) t -> p b t", p=128), [mgb[st]], [big.b], big.b)
            for j in range(len(sp["wo"].blocks)):
                slot, bw, kc = wload(sp["wo"], j)
                for si in range(2):
                    ps = ybank()
                    dense(slot, bw, kc, si * 128, 128, big, TS, TS, ps)
                    residual_out(ps, xin, xinb[st], x1, x1b[st], 2 * j + si, t0, fuse=(g0 + c.g_lnmem, "hT", st))
            slot_id = t0 // c.HALF
            for j in range(len(sp["mq"].blocks)):
                slot, bw, kc = wload(sp["mq"], j)
                for si in range(2):
                    hh = 2 * j + si
                    ps = ybank()
                    dense(slot, bw, kc, si * 128, 128, hT, TS, TS, ps)
                    epi_hn(ps, g0 + c.g_mem, None, 0, dst_sb=qm, dst_off=hh * TS)
            for hh in range(c.MEMH):
                nmc = c.MEM // 128
                kcs = [(kc, None, None, zcol, None) for kc in range(nmc)]
                base = hh * 512 + slot_id * c.MEM
                attn_qtile([(qm, hh * TS, 128)],
                           lambda kc: [(KTm, base + kc * 128, 128)],
                           kcs, lambda kc, v: (Vm, Vm[:, base + kc * 128: base + (kc + 1) * 128]),
                           1, SC128, [PS[6]], PS[5], ptm, None, [PS[4]])
                kb.op("dve", lambda e: e.reciprocal(out=recm[:, :], in_=PS[5][:, :]), reads=[PS[5].b], writes=[recm.b])
                kb.op("dve", lambda e: e.tensor_tensor(out=om_[:, hh * TS:(hh + 1) * TS], in0=PS[6][:, :], in1=recm[:, :], op=ALU.mult),
                      reads=[PS[6].b, recm.b], writes=[om_.b])
            for j in range(len(sp["mo"].blocks)):
                slot, bw, kc = wload(sp["mo"], j)
                for si in range(2):
                    ps = ybank()
                    dense(slot, bw, kc, si * 128, 128, om_, TS, TS, ps)
                    residual_out(ps, x1, x1b[st], x2, x2b[st], 2 * j + si, t0, fuse=(g0 + c.g_lnffn, "hT", st))
            for hf in range(2):
                k0 = hf * c.FCH
                nk = min(c.FCH, c.FC - k0)
                for jj in range(nk):
                    slot, bw, kc = wload(sp["fi"], k0 + jj)
                    pg = ybank(); pu = ybank()
                    dense(slot, bw, kc, 0, 128, hT, TS, TS, pg)
                    dense(slot, bw, kc, 128, 128, hT, TS, TS, pu)
                    sg = sg_r.next()
                    kb.op("act", lambda e: e.activation(out=sg[:, :], in_=pg[:, :], func=AF.Silu), reads=[pg.b], writes=[sg.b])
                    kb.op("dve", lambda e: e.tensor_tensor(out=big[:, jj * TS:(jj + 1) * TS], in0=pu[:, :], in1=sg[:, :], op=ALU.mult),
                          reads=[pu.b, sg.b], writes=[big.b])
                xs_, xsb_ = (x2, x2b[st]) if hf == 0 else (x2h, x2hb[st])
                xd_, xdb_ = (x2h, x2hb[st]) if hf == 0 else (x3, x3b[st])
                for ob in range(NC):
                    slot, bw, kc = wload(sp["fo"], hf * NC + ob)
                    ps = ybank()
                    dense(slot, bw, kc, 0, 128, big, TS, TS, ps)
                    fz = ((l + 1) * c.NG + c.g_lnmix, "dram", st) if (hf == 1 and l < L - 1) else None
                    residual_out(ps, xs_, xsb_, xd_, xdb_, ob, t0, fuse=fz)
        sC.close()

    kb.barrier()
    kb.es.close()
    return nc


def tab_widths(c):
    WW = 5 * 128 + 512
    WD = 19 * 128 + 512
    WL = 7 * 128 + 512
    relmax = c.NCH - 1; relmin = -(c.NCH - 4)
    WA = (relmax - relmin) * 128 + 512
    return WW, WD, WL, WA


def make_tables(c):
    WW, WD, WL, WA = tab_widths(c)
    i = np.arange(128)[:, None].astype(np.float64)
    sc = 128.0 ** -0.5

    def wide(width, relmax, f):
        uu = np.arange(width)[None, :].astype(np.float64)
        return f(i - uu + relmax * 128)

    def f_win(d):
        return np.where(np.abs(d) <= 128, -np.abs(d), NEG)

    def mult_of(d):
        ad = np.abs(d)
        return ((ad <= 64).astype(np.float64) + ((ad <= 256) & (np.mod(d, 4) == 0))
                + ((ad <= 1024) & (np.mod(d, 16) == 0)))

    def f_dil(d):
        return np.where(mult_of(d) > 0, -np.abs(d), NEG)

    def f_lm(d):
        m = mult_of(d)
        return np.where(m > 1, np.log(np.maximum(m, 1.0)), 0.0) / sc

    t_win = wide(WW, 4, f_win).astype(np.float32)
    t_dil = wide(WD, 11, f_dil).astype(np.float32)
    t_lm = wide(WL, 5, f_lm).astype(np.float32)
    t_abs = wide(WA, c.NCH - 1, np.abs).astype(np.int16)
    return t_win, t_dil, t_lm, t_abs


def make_gt(c, p, l):
    cols = []
    def chunks(v):
        v = np.asarray(v, np.float32).reshape(-1, 128)
        for r in v:
            cols.append(r)
    def col64(v):
        z = np.zeros(128, np.float32); z[:64] = v; cols.append(z)
    chunks(p["ln_mix_g"][l]); chunks(p["ln_mem_g"][l]); chunks(p["mem_ln_g"][l]); chunks(p["ln_ffn_g"][l])
    chunks(p["mla_qa_g"][l]); chunks(p["mla_kva_g"][l])
    g = p["mla_qk_g"][l]
    cols.append(g[0, :128]); col64(g[0, 128:]); cols.append(g[1, :128]); col64(g[1, 128:])
    chunks(p["dil_qk_g"][l]); chunks(p["win_qk_g"][l]); chunks(p["diff_qk_g"][l]); chunks(p["mem_qk_g"][l])
    chunks(p["diff_subln_g"][l]); chunks(p["diff_lambda"][l])
    for h in range(c.H):
        cols.append(np.full(128, p["win_sink"][l][h], np.float32))
    gt = np.stack(cols, axis=1).astype(np.float32)
    assert gt.shape == (128, c.NG), gt.shape
    return gt


def rope_tables(pos):
    half = 32
    inv = (10000.0 ** (-(np.arange(half, dtype=np.float32)) / half)).astype(np.float32)
    ang = (pos.astype(np.float32)[None, :] * inv[:, None]).astype(np.float32).astype(np.float64)
    cs = np.cos(ang); sn = np.sin(ang)
    cosT = np.concatenate([cs, cs], axis=0).astype(np.float32)
    sinT = np.concatenate([-sn, sn], axis=0).astype(np.float32)
    return cosT, sinT


_CACHE = {}


def run(c, inputs, debug=False):
    p = {k: np.asarray(v) for k, v in inputs.items()}
    xp, xsm = p["x_prompt"], p["x_sample"]
    mp, ms = p["mem_prompt"], p["mem_sample"]
    nP = xp.shape[0]
    S = xp.shape[1]; S2 = xsm.shape[1]
    assert S == c.TOK and 2 * S2 == c.TOK
    key = (c.D, c.TOK, debug)
    if key not in _CACHE:
        _CACHE[key] = build(c, debug)
    nc = _CACHE[key]
    t_win, t_dil, t_lm, t_abs = make_tables(c)
    gt = np.concatenate([make_gt(c, p, l) for l in range(c.depth)], axis=1)
    ident = np.eye(128, dtype=np.float32)
    cos_p, sin_p = rope_tables(np.arange(S))
    pos_s = np.concatenate([np.arange(S2), np.arange(S2)])
    cos_s, sin_s = rope_tables(pos_s)
    wnames = ["w_in", "mla_wq_up", "mla_wkv_up", "w_branch", "w_out", "mem_wq", "mem_wkv", "mem_wo", "ffn_w_in", "ffn_w_out"]
    shared = {k: np.ascontiguousarray(p[k], dtype=np.float32) for k in wnames}
    shared.update({"gt": gt, "tab_win": t_win, "tab_dil": t_dil, "tab_lm": t_lm, "tab_abs": t_abs, "ident_in": ident})
    in_maps = []
    ncores = nP + xsm.shape[0] // 2
    for core in range(ncores):
        m = dict(shared)
        if core < nP:
            m["xT"] = np.ascontiguousarray(xp[core].T)
            mt = mp[core].T
            m["memT"] = np.ascontiguousarray(np.concatenate([mt, mt], axis=1))
            m["cosT"], m["sinT"] = cos_p, sin_p
            m["bmask"] = np.zeros((128, 2), np.float32)
        else:
            b0 = 2 * (core - nP)
            m["xT"] = np.ascontiguousarray(np.concatenate([xsm[b0].T, xsm[b0 + 1].T], axis=1))
            m["memT"] = np.ascontiguousarray(np.concatenate([ms[b0].T, ms[b0 + 1].T], axis=1))
            m["cosT"], m["sinT"] = cos_s, sin_s
            bmk = np.zeros((128, 2), np.float32); bmk[:, 1] = NEG
            m["bmask"] = bmk
        in_maps.append(m)
    res = run_bass_kernel_spmd(nc, in_maps, core_ids=list(range(ncores)))
    outs = res.results
    yp = np.stack([np.ascontiguousarray(outs[i]["yT"].T) for i in range(nP)], axis=0)
    ysl = []
    for core in range(nP, ncores):
        y = outs[core]["yT"].T
        ysl.append(y[:S2]); ysl.append(y[S2:])
    ys = np.stack([np.ascontiguousarray(a) for a in ysl], axis=0)
    return (yp.astype(np.float32), ys.astype(np.float32)), outs


def kernel(**inputs):
    c = Cfg()
    (yp, ys), _ = run(c, inputs)
    return (yp, ys)
```

```python
import math
from contextlib import ExitStack
import numpy as np
import concourse.bass as bass
import concourse.mybir as mybir
from concourse.bass_utils import run_bass_kernel_spmd

F32 = mybir.dt.float32
BF16 = mybir.dt.bfloat16
AF = mybir.ActivationFunctionType
ALU = mybir.AluOpType
EPS = 1e-6
NEG = -1e30


class Cfg:
    def __init__(s, D=4096, TOK=4096, MEM=256, depth=2):
        s.D = D; s.TOK = TOK; s.MEM = MEM; s.depth = depth
        s.NC = D // 128
        s.TS = 512
        s.NST = TOK // s.TS
        s.NCH = TOK // 128
        s.HALF = TOK // 2
        s.BW = D // 4
        s.H = s.BW // 128
        s.WKV = s.H // 4
        s.DH = s.BW // 256
        s.QL = D // 4; s.KVL = D // 8
        s.FH = -(-8 * D // (3 * 256)) * 256
        s.FC = s.FH // 128
        s.MEMH = 4; s.MEMW = 512
        s.MIX = s.QL + s.KVL + 64 + 3 * s.BW + s.BW + 2 * s.WKV * 128 + 3 * s.BW
        s.INC = s.MIX + 4 * D
        o = 0
        s.o_qa = o; o += s.QL
        s.o_kva = o; o += s.KVL
        s.o_kpe = o; o += 64
        s.o_dq = o; o += s.BW
        s.o_dk = o; o += s.BW
        s.o_dv = o; o += s.BW
        s.o_wq = o; o += s.BW
        s.o_wk = o; o += s.WKV * 128
        s.o_wv = o; o += s.WKV * 128
        s.o_fq = o; o += s.BW
        s.o_fk = o; o += s.BW
        s.o_fv = o; o += s.BW
        assert o == s.MIX
        i = 0
        s.q_mla_n = i; i += s.H
        s.k_mla_n = i; i += s.H
        s.q_mla_r = i; i += s.H
        s.k_mla_r = i; i += s.H
        s.q_dil = i; i += s.H
        s.k_dil = i; i += s.H
        s.q_win = i; i += s.H
        s.k_win = i; i += s.WKV
        s.q_dif = i; i += 2 * s.DH
        s.k_dif = i; i += 2 * s.DH
        s.nQK = i
        i = 0
        s.v_mla = i; i += s.H
        s.v_dil = i; i += s.H
        s.v_win = i; i += s.WKV
        s.v_dif = i; i += 2 * s.DH
        s.nV = i
        g = 0
        s.g_lnmix = g; g += s.NC
        s.g_lnmem = g; g += s.NC
        s.g_memln = g; g += s.NC
        s.g_lnffn = g; g += s.NC
        s.g_qa = g; g += s.QL // 128
        s.g_kva = g; g += s.KVL // 128
        s.g_mla = g; g += 4
        s.g_dil = g; g += 2
        s.g_win = g; g += 2
        s.g_dif = g; g += 2
        s.g_mem = g; g += 2
        s.g_subln = g; g += 2
        s.g_lam = g; g += 4
        s.g_sink = g; g += s.H
        s.NG = g


def alibi(n):
    return [2.0 ** (-8.0 * (i + 1) / n) for i in range(n)]


class Buf:
    __slots__ = ("name", "lw", "rd", "dsem", "dcnt", "multi")

    def __init__(s, name, multi=False):
        s.name = name; s.lw = {}; s.rd = {}; s.dsem = None; s.dcnt = 0; s.multi = multi


class KB:
    def __init__(s, nc):
        s.nc = nc
        s.es = ExitStack()
        s.E = {"pe": nc.tensor, "act": nc.scalar, "dve": nc.vector, "pool": nc.gpsimd, "sp": nc.sync}
        s.sem = {e: s.es.enter_context(nc.semaphore("e_" + e)) for e in ("pe", "act", "dve", "pool")}
        s.cnt = {e: 0 for e in s.sem}
        s.waited = {e: {} for e in s.E}
        s.allsems = {}
        s.nsem = 4
        s.sem_pool = []

    def _wait(s, eng, ev, skip=None):
        sem, val = ev
        if eng == "pe" and sem is s.sem["pe"]:
            return
        if skip is not None and sem is skip:
            return
        w = s.waited[eng]
        if w.get(sem.num, 0) < val:
            s.E[eng].wait_ge(sem, val)
            w[sem.num] = val

    def _deps(s, eng, reads, writes, skip=None):
        for b in reads:
            for ev in b.lw.values():
                s._wait(eng, ev, skip)
        for b in writes:
            if not b.multi:
                for ev in b.lw.values():
                    s._wait(eng, ev, skip)
            for ev in b.rd.values():
                s._wait(eng, ev, skip)

    def _commit(s, ev, reads, writes):
        for b in reads:
            b.rd[ev[0].num] = ev
        for b in writes:
            if b.multi and not b.rd:
                b.lw[ev[0].num] = ev
            else:
                b.lw = {ev[0].num: ev}
            b.rd = {}
        s.allsems[ev[0].num] = ev

    def op(s, eng, fn, reads=(), writes=()):
        s._deps(eng, reads, writes)
        ins = fn(s.E[eng])
        s.cnt[eng] += 1
        ins.then_inc(s.sem[eng], 1)
        s._commit((s.sem[eng], s.cnt[eng]), reads, writes)

    def dma(s, q, out, in_, reads, writes, sb):
        if sb.dsem is None:
            if s.sem_pool:
                sb.dsem, sb.dcnt = s.sem_pool.pop()
            else:
                sb.dsem = s.es.enter_context(s.nc.semaphore("d_" + sb.name))
                s.nsem += 1
                sb.dcnt = 0
        s._deps(q, reads, writes, skip=sb.dsem)
        ins = s.E[q].dma_start(out=out, in_=in_)
        sb.dcnt += 16
        ins.then_inc(sb.dsem, 16)
        s._commit((sb.dsem, sb.dcnt), reads, writes)

    def barrier(s):
        if getattr(s, "pre_barrier", None) is not None:
            s.pre_barrier()
        for eng in s.E:
            for ev in list(s.allsems.values()):
                s._wait(eng, ev)


class Scope:
    n = 0

    def __init__(s, kb):
        s.kb = kb; s.es = ExitStack(); s.bufs = []
        Scope.n += 1; s.id = Scope.n

    def T(s, name, shape, dt, psum=False):
        return T(s.kb, f"{name}_s{s.id}", shape, dt, psum=psum, scope=s)

    def ring(s, name, n, shape, dt):
        r = Ring.__new__(Ring)
        r.tiles = [s.T(f"{name}{i}", shape, dt) for i in range(n)]; r.i = 0
        return r

    def buf(s, name):
        b = Buf(f"{name}_s{s.id}"); s.bufs.append(b); return b

    def close(s):
        s.kb.barrier()
        for b in s.bufs:
            if b.dsem is not None:
                s.kb.sem_pool.append((b.dsem, b.dcnt)); b.dsem = None
        s.es.close()


class T:
    def __init__(s, kb, name, shape, dt, psum=False, scope=None):
        s.b = Buf(name)
        alloc = kb.nc.psum_tensor if psum else kb.nc.sbuf_tensor
        es = scope.es if scope is not None else kb.es
        s.t = es.enter_context(alloc(name, list(shape), dt))
        if scope is not None:
            scope.bufs.append(s.b)

    def __getitem__(s, k):
        return s.t[k]


class Ring:
    def __init__(s, kb, name, n, shape, dt):
        s.tiles = [T(kb, f"{name}{i}", shape, dt) for i in range(n)]
        s.i = 0

    def next(s):
        t = s.tiles[s.i % len(s.tiles)]
        s.i += 1
        return t


class WSpec:
    def __init__(s, name, kc, bw, blocks):
        s.name = name; s.kc = kc; s.bw = bw; s.blocks = blocks; s.dram = None; s.buf = None


def build_specs(c):
    sp = {}
    D = c.D

    def seg(src, col0, w, sub=None, row0=0):
        return (src, sub, row0, col0, w)

    blocks = []
    kinds = []
    def add_range(o, n, kindf):
        nb = n // 256
        for j in range(nb):
            blocks.append([seg("w_in", o + j * 256, 256)])
            kinds.append([kindf(2 * j), kindf(2 * j + 1)])
    add_range(c.o_qa, c.QL, lambda i: ("latq", i))
    add_range(c.o_kva, c.KVL, lambda i: ("latkv", i))
    blocks.append([seg("w_in", c.o_kpe, 64), seg("w_in", c.o_kpe + 32, 32), seg("w_in", c.o_kpe, 32)])
    kinds.append([("kpe", 0)])
    add_range(c.o_dq, c.BW, lambda i: ("hn", c.q_dil + i, c.g_dil))
    add_range(c.o_dk, c.BW, lambda i: ("hn", c.k_dil + i, c.g_dil + 1))
    add_range(c.o_dv, c.BW, lambda i: ("v", c.v_dil + i))
    add_range(c.o_wq, c.BW, lambda i: ("hn", c.q_win + i, c.g_win))
    if c.WKV * 128 >= 256:
        add_range(c.o_wk, c.WKV * 128, lambda i: ("hn", c.k_win + i, c.g_win + 1))
        add_range(c.o_wv, c.WKV * 128, lambda i: ("v", c.v_win + i))
    else:
        blocks.append([seg("w_in", c.o_wk, 128), seg("w_in", c.o_wv, 128)])
        kinds.append([("hn", c.k_win, c.g_win + 1), ("v", c.v_win)])
    add_range(c.o_fq, c.BW, lambda i: ("hn", c.q_dif + i, c.g_dif))
    add_range(c.o_fk, c.BW, lambda i: ("hn", c.k_dif + i, c.g_dif + 1))
    add_range(c.o_fv, c.BW, lambda i: ("v", c.v_dif + i))
    s = WSpec("wa", c.NC, 256, blocks); s.kinds = kinds
    sp["wa"] = s
    sp["wqu"] = WSpec("wqu", c.QL // 128, 256,
                      [[seg("mla_wq_up", h * 192, 192), seg("mla_wq_up", h * 192 + 160, 32),
                        seg("mla_wq_up", h * 192 + 128, 32)] for h in range(c.H)])
    sp["wkvu"] = WSpec("wkvu", c.KVL // 128, 256, [[seg("mla_wkv_up", h * 256, 256)] for h in range(c.H)])
    gb = []
    for ob in range(c.NC):
        for pr in range(2):
            gb.append([seg("w_in", c.MIX + (2 * pr) * D + ob * 128, 128),
                       seg("w_in", c.MIX + (2 * pr + 1) * D + ob * 128, 128)])
    sp["wg"] = WSpec("wg", c.NC, 256, gb)
    sp["wb"] = WSpec("wb", c.BW // 128, 512,
                     [[seg("w_branch", ob * 128, 128, sub=i) for i in range(4)] for ob in range(c.NC)])
    sp["wo"] = WSpec("wo", c.NC, 256, [[seg("w_out", j * 256, 256)] for j in range(D // 256)])
    sp["mq"] = WSpec("mq", c.NC, 256, [[seg("mem_wq", j * 256, 256)] for j in range(c.MEMW // 256)])
    sp["mkv"] = WSpec("mkv", c.NC, 256, [[seg("mem_wkv", j * 256, 256)] for j in range(2 * c.MEMW // 256)])
    sp["mo"] = WSpec("mo", c.MEMW // 128, 256, [[seg("mem_wo", j * 256, 256)] for j in range(D // 256)])
    sp["fi"] = WSpec("fi", c.NC, 256, [[seg("ffn_w_in", j * 128, 128), seg("ffn_w_in", c.FH + j * 128, 128)]
                                       for j in range(c.FC)])
    c.FCH = (c.FC + 1) // 2
    fo = []
    for hf in range(2):
        k0 = hf * c.FCH
        nk = min(c.FCH, c.FC - k0)
        for ob in range(c.NC):
            fo.append([seg("ffn_w_out", ob * 128, 128, row0=k0 * 128)] + [("__nk__", nk)])
    s = WSpec("fo", c.FCH, 128, fo)
    sp["fo"] = s
    return sp


WSLOT = 8192


def build(c, debug=False):
    nc = bass.Bass("TRN2", target_bir_lowering=False)
    kb = KB(nc)
    D, TOK, TS, NC = c.D, c.TOK, c.TS, c.NC
    L = c.depth

    def din(name, shape, dt=F32):
        return nc.dram_tensor(name, list(shape), dt, kind="ExternalInput").ap()

    def dscr(name, shape, dt):
        kind = "ExternalOutput" if debug else "Internal"
        return nc.dram_tensor(name, list(shape), dt, kind=kind).ap()

    xT = din("xT", [D, TOK]); memT = din("memT", [D, 2 * c.MEM])
    W = {
        "w_in": din("w_in", [L, D, c.INC]), "mla_wq_up": din("mla_wq_up", [L, c.QL, c.H * 192]),
        "mla_wkv_up": din("mla_wkv_up", [L, c.KVL, c.H * 256]), "w_branch": din("w_branch", [L, 4, c.BW, D]),
        "w_out": din("w_out", [L, D, D]), "mem_wq": din("mem_wq", [L, D, c.MEMW]),
        "mem_wkv": din("mem_wkv", [L, D, 2 * c.MEMW]), "mem_wo": din("mem_wo", [L, c.MEMW, D]),
        "ffn_w_in": din("ffn_w_in", [L, D, 2 * c.FH]), "ffn_w_out": din("ffn_w_out", [L, c.FH, D]),
    }
    GT_d = din("gt", [128, L * c.NG])
    cos_d = din("cosT", [64, TOK]); sin_d = din("sinT", [64, TOK])
    bm_d = din("bmask", [128, 2])
    ident_d = din("ident_in", [128, 128])
    WW, WD, WL, WA = tab_widths(c)
    tabw_d = din("tab_win", [128, WW]); tabd_d = din("tab_dil", [128, WD]); tabl_d = din("tab_lm", [128, WL])
    taba_d = din("tab_abs", [128, WA], mybir.dt.int16)
    yT = nc.dram_tensor("yT", [D, TOK], F32, kind="ExternalOutput").ap()

    specs = [build_specs(c) for _ in range(L)]
    for l in range(L):
        for s in specs[l].values():
            s.dram = dscr(f"wb_{s.name}_{l}", [len(s.blocks) * 128, s.kc * s.bw], BF16)
            s.bufs = [Buf(f"wb_{s.name}_{l}_{j}", True) for j in range(len(s.blocks))]
    QK = dscr("qk", [c.nQK * 128, TOK], BF16); QKb = [Buf(f"qk{i}", True) for i in range(c.nQK)]
    VS = dscr("vs", [c.nV * 128, c.NCH * 128], BF16); VSb = [Buf(f"vs{i}", True) for i in range(c.nV)]
    OT = dscr("ot", [4 * c.BW, TOK], BF16); OTb = [Buf(f"ot{i}", True) for i in range(4 * c.BW // 128)]
    MG = dscr("mg", [D, TOK], BF16)
    X = [xT] + [dscr(f"x{i}", [D, TOK], F32) for i in range(1, 4 * L)] + [yT]
    Xb = [[Buf(f"X{i}_{st}", True) for st in range(c.NST)] for i in range(4 * L + 1)]
    HG = [dscr(f"hg{l}", [D, TOK], BF16) for l in range(L)]
    HGb = [[Buf(f"hg{l}_{st}", True) for st in range(c.NST)] for l in range(L)]
    RS = [dscr(f"rs{l}", [c.NST * 128, TS], F32) for l in range(L)]
    RSb = [[Buf(f"rs{l}_{st}", True) for st in range(c.NST)] for l in range(L)]

    ident = T(kb, "ident", [128, 128], BF16)
    ones = {}
    for d in (c.D, c.QL, c.KVL, 256, 128, 64):
        if d not in ones:
            ones[d] = T(kb, f"ones{d}", [128, 128], BF16)
    onesv = T(kb, "onesv", [128, 128], BF16)
    GT = T(kb, "GT", [128, L * c.NG], F32)
    bm = T(kb, "bm", [128, 2], F32)
    esink = T(kb, "esink", [128, L * c.H], F32)
    lam = T(kb, "lam", [128, 4 * L], F32)
    epst = T(kb, "epst", [128, 2], F32)
    xs = Ring(kb, "xs", 3, [128, TS], F32)
    sqr = Ring(kb, "sq", 3, [128, TS], BF16)
    rstd = T(kb, "rstd", [128, TS], F32)
    sdt = T(kb, "sdt", [128, TS], F32)
    ost = Ring(kb, "ost", 3, [128, TS], F32)
    bst = Ring(kb, "bst", 5, [128, TS], BF16)
    KTm = T(kb, "KTm", [128, c.MEMH * 512], BF16)
    rs2 = T(kb, "rs2", [128, TS], F32)
    CVN = 2048
    cstg = Ring(kb, "cstg", 2, [128, CVN], F32)
    ccv = Ring(kb, "ccv", 2, [128, CVN], BF16)
    Vm = T(kb, "Vm", [128, c.MEMH * 512], BF16)
    PS = [T(kb, f"ps{i}", [128, 512], F32, psum=True) for i in range(7)]
    psb_h = {}
    psrot = [0]

    def ybank():
        t = PS[psrot[0] % 4]
        psrot[0] += 1
        return t

    def mset(t, v):
        kb.op("dve", lambda e: e.memset(t[:], v), writes=[t.b])
    for d, t in ones.items():
        mset(t, 1.0 / d)
    mset(onesv, 1.0)
    kb.op("dve", lambda e: e.memset(epst[:, 0:1], EPS), writes=[epst.b])
    kb.op("dve", lambda e: e.memset(epst[:, 1:2], 3 * EPS), writes=[epst.b])
    epsc = {1: epst[:, 0:1], 3: epst[:, 1:2]}
    kb.dma("sp", GT[:], GT_d[:, :], [], [GT.b], GT.b)
    kb.dma("sp", bm[:], bm_d[:, :], [], [bm.b], bm.b)
    zcol = bm[:, 0:1]
    mcol = bm[:, 1:2]

    s0 = Scope(kb)
    identf = s0.T("identf", [128, 128], F32)
    kb.dma("sp", identf[:], ident_d[:, :], [], [identf.b], identf.b)
    kb.op("dve", lambda e: e.tensor_copy(out=ident[:], in_=identf[:]), reads=[identf.b], writes=[ident.b])
    onesf = s0.T("onesf", [128, 128], F32)
    mset(onesf, 1.0)
    for l in range(L):
        g0 = l * c.NG
        kb.op("act", lambda e: e.activation(out=esink[:, l * c.H:(l + 1) * c.H],
                                            in_=GT[:, g0 + c.g_sink:g0 + c.g_sink + c.H], func=AF.Exp),
              reads=[GT.b], writes=[esink.b])
        lt = s0.T(f"lamtmp{l}", [128, 4], F32)
        for j in range(2):
            kb.op("dve", lambda e: e.tensor_tensor(out=lt[:, j:j + 1], in0=GT[:, g0 + c.g_lam + 2 * j:g0 + c.g_lam + 2 * j + 1],
                                                   in1=GT[:, g0 + c.g_lam + 2 * j + 1:g0 + c.g_lam + 2 * j + 2], op=ALU.mult),
                  reads=[GT.b], writes=[lt.b])
        pl = PS[6]
        kb.op("pe", lambda e: e.matmul(pl[:, 0:2], onesf[:], lt[:, 0:2], start=True, stop=True),
              reads=[onesf.b, lt.b], writes=[pl.b])
        kb.op("act", lambda e: e.activation(out=lt[:, 2:4], in_=pl[:, 0:2], func=AF.Exp), reads=[pl.b], writes=[lt.b])
        lam_init = 0.8 - 0.6 * math.exp(-0.3 * l)
        kb.op("dve", lambda e: e.scalar_tensor_tensor(out=lam[:, 4 * l:4 * l + 1], in0=lt[:, 2:3], scalar=lam_init,
                                                      in1=lt[:, 3:4], op0=ALU.add, op1=ALU.subtract),
              reads=[lt.b], writes=[lam.b])
        kb.op("dve", lambda e: e.tensor_scalar(out=lam[:, 4 * l + 1:4 * l + 2], in0=lam[:, 4 * l:4 * l + 1],
                                               scalar1=-1.0, scalar2=None, op0=ALU.mult),
              reads=[lam.b], writes=[lam.b])

    order0 = ["wa", "wqu", "wkvu"]
    order1 = ["mkv", "wg", "wb", "wo", "mq", "mo", "fi", "fo"]
    units = []
    done_at = {}
    for l in range(L):
        for nm in order0 + order1:
            s = specs[l][nm]
            for j, blk in enumerate(s.blocks):
                kc = s.kc
                segs = blk
                if segs[-1][0] == "__nk__":
                    kc = segs[-1][1]; segs = segs[:-1]
                bw = sum(sg[4] for sg in segs)
                step = max(1, CVN // bw)
                for k0 in range(0, kc, step):
                    units.append((l, s, j, k0, min(step, kc - k0), segs, bw))
            done_at[(l, nm)] = len(units)
    cst = {"ld": 0, "cv": 0, "sync": True, "pend": []}

    def conv_load():
        u = cst["ld"]
        l, s, j, k0, nk, segs, bw = units[u]
        st_ = cstg.next()
        off = 0
        sview = st_[:, 0:nk * bw].rearrange("p (k c) -> p k c", k=nk)
        q = "sp" if cst["sync"] else "act"
        for (srcn, sub, row0, col0, w) in segs:
            wd = W[srcn]
            r0 = row0 + k0 * 128
            if sub is None:
                a = wd[l, r0:r0 + nk * 128, col0:col0 + w]
            else:
                a = wd[l, sub, r0:r0 + nk * 128, col0:col0 + w]
            a = a.rearrange("(k p) c -> p k c", p=128)
            kb.dma(q, sview[:, :, off:off + w], a, [], [st_.b], st_.b)
            off += w
        cst["pend"].append((u, st_))
        cst["ld"] += 1

    def conv_cast():
        u, st_ = cst["pend"].pop(0)
        l, s, j, k0, nk, segs, bw = units[u]
        cv_ = ccv.next()
        n_ = nk * bw
        ce = ["dve", "act", "pool"][u % 3] if cst["sync"] else "act"
        if ce == "act":
            kb.op("act", lambda e: e.activation(out=cv_[:, 0:n_], in_=st_[:, 0:n_], func=AF.Copy),
                  reads=[st_.b], writes=[cv_.b])
        else:
            kb.op(ce, lambda e: e.tensor_copy(out=cv_[:, 0:n_], in_=st_[:, 0:n_]), reads=[st_.b], writes=[cv_.b])
        kb.dma("pool" if cst["sync"] else "act", s.dram[j * 128:(j + 1) * 128, k0 * bw:k0 * bw + n_], cv_[:, 0:n_],
               [cv_.b], [s.bufs[j]], cv_.b)
        cst["cv"] += 1

    def conv_step():
        if cst["cv"] >= len(units):
            return
        if len(cst["pend"]) >= 2 or cst["ld"] >= len(units):
            conv_cast()
        if cst["ld"] < len(units):
            conv_load()

    def pump(n=1):
        for _ in range(n):
            conv_step()

    def ensure(l, names):
        tgt = max(done_at[(l, nm)] for nm in names)
        while cst["cv"] < tgt:
            conv_step()

    ensure(0, order0)
    cst["sync"] = False
    s0.close()

    cur = {}

    def wload(spec, j, q="sp"):
        slot = cur["wring"].next()
        kc = spec.kc
        blk = spec.blocks[j]
        if blk[-1][0] == "__nk__":
            kc = blk[-1][1]
            bw = sum(sg[4] for sg in blk[:-1])
        else:
            bw = sum(sg[4] for sg in blk)
        nn = kc * bw
        assert nn <= WSLOT, (spec.name, nn)
        kb.dma(q, slot[:, 0:nn], spec.dram[j * 128:(j + 1) * 128, 0:nn], [spec.bufs[j]], [slot.b], slot.b)
        pump(cur.get("rate", 1))
        return slot, bw, kc

    def mm_group(ps_ap, psb, pairs, reads):
        def fn(e):
            ins = None
            nn = len(pairs)
            for i, (a, b) in enumerate(pairs):
                ins = e.matmul(ps_ap, a, b, start=(i == 0), stop=(i == nn - 1))
            return ins
        kb.op("pe", fn, reads=reads, writes=[psb])

    pend = []

    def defer(fn):
        pend.append(fn)

    def run_deferred():
        while pend:
            pend.pop(0)()
    kb.pre_barrier = run_deferred

    def rstd_from(ss_ps, eps_eff, ntok=TS):
        kb.op("act", lambda e: e.activation(out=sdt[:, 0:ntok], in_=ss_ps[:, 0:ntok], func=AF.Ln, bias=epsc[eps_eff], scale=1.0),
              reads=[ss_ps.b, epst.b], writes=[sdt.b])
        kb.op("act", lambda e: e.activation(out=rstd[:, 0:ntok], in_=sdt[:, 0:ntok], func=AF.Exp, scale=-0.5),
              reads=[sdt.b], writes=[rstd.b])

    def recip_act(dst, den_ps, bias_ap=None, extra=()):
        if bias_ap is None:
            kb.op("act", lambda e: e.activation(out=dst[:, :], in_=den_ps[:, :], func=AF.Ln), reads=[den_ps.b], writes=[dst.b])
        else:
            kb.op("act", lambda e: e.activation(out=dst[:, :], in_=den_ps[:, :], func=AF.Ln, bias=bias_ap, scale=1.0),
                  reads=[den_ps.b] + list(extra), writes=[dst.b])
        kb.op("act", lambda e: e.activation(out=dst[:, :], in_=dst[:, :], func=AF.Exp, scale=-1.0), reads=[dst.b], writes=[dst.b])

    def norm_stream(xsrc, xbuf, t0, ntok, gcol0):
        hT = cur["hT"]
        onesD = ones[c.D]
        ssp = PS[4]
        for cc in range(NC):
            xt = xs.next()
            kb.dma("pool", xt[:, 0:ntok], xsrc[cc * 128:(cc + 1) * 128, t0:t0 + ntok], [xbuf], [xt.b], xt.b)
            sq = sqr.next()
            kb.op("act", lambda e: e.activation(out=sq[:, 0:ntok], in_=xt[:, 0:ntok], func=AF.Square),
                  reads=[xt.b], writes=[sq.b])
            kb.op("pe", lambda e: e.matmul(ssp[:, 0:ntok], onesD[:], sq[:, 0:ntok], start=(cc == 0), stop=(cc == NC - 1)),
                  reads=[onesD.b, sq.b], writes=[ssp.b])
        rstd_from(ssp, 1, ntok)
        for cc in range(NC):
            xt = xs.next()
            kb.dma("pool", xt[:, 0:ntok], xsrc[cc * 128:(cc + 1) * 128, t0:t0 + ntok], [xbuf], [xt.b], xt.b)
            kb.op("dve", lambda e: e.scalar_tensor_tensor(out=hT[:, cc * TS:cc * TS + ntok], in0=xt[:, 0:ntok],
                                                          scalar=GT[:, gcol0 + cc:gcol0 + cc + 1], in1=rstd[:, 0:ntok],
                                                          op0=ALU.mult, op1=ALU.mult),
                  reads=[xt.b, GT.b, rstd.b], writes=[hT.b])

    def hsave(l, st):
        hT = cur["hT"]; t0 = st * TS
        kb.dma("pool", HG[l][:, t0:t0 + TS].rearrange("(b p) t -> p b t", p=128),
               hT[:, 0:NC * TS].rearrange("p (b t) -> p b t", b=NC), [hT.b], [HGb[l][st]], hT.b)

    def hscale(src_t):
        hT = cur["hT"]
        for cc in range(NC):
            eng = "dve" if cc % 4 != 3 else "pool"
            kb.op(eng, lambda e: e.tensor_tensor(out=hT[:, cc * TS:(cc + 1) * TS], in0=hT[:, cc * TS:(cc + 1) * TS],
                                                 in1=src_t[:, :], op=ALU.mult),
                  reads=[hT.b, src_t.b], writes=[hT.b])

    def hload(l, st, scaled):
        hT = cur["hT"]; t0 = st * TS
        kb.dma("sp", hT[:, 0:NC * TS].rearrange("p (b t) -> p b t", b=NC),
               HG[l][:, t0:t0 + TS].rearrange("(b p) t -> p b t", p=128), [HGb[l][st]], [hT.b], hT.b)
        if scaled:
            kb.dma("sp", rs2[:, :], RS[l][st * 128:(st + 1) * 128, :], [RSb[l][st]], [rs2.b], rs2.b)
            hscale(rs2)

    def dense(slot, bw, kc, col0, ncols, rhs_t, rhs_stride, ntok, ps, rhs_off=0):
        pairs = [(slot[:, k * bw + col0:k * bw + col0 + ncols],
                  rhs_t[:, rhs_off + k * rhs_stride:rhs_off + k * rhs_stride + ntok]) for k in range(kc)]
        mm_group(ps[0:ncols, 0:ntok], ps.b, pairs, [slot.b, rhs_t.b])
        run_deferred()

    def epi_hn(ps, gcol, qkid, t0, ntok=TS, dst_sb=None, dst_off=0):
        sq = sqr.next()
        kb.op("act", lambda e: e.activation(out=sq[:, 0:ntok], in_=ps[:, 0:ntok], func=AF.Square), reads=[ps.b], writes=[sq.b])

        def tail():
            ssp = PS[5]
            kb.op("pe", lambda e: e.matmul(ssp[:, 0:ntok], ones[128][:], sq[:, 0:ntok], start=True, stop=True),
                  reads=[ones[128].b, sq.b], writes=[ssp.b])
            rstd_from(ssp, 1, ntok)
            if dst_sb is None:
                o = bst.next()
                kb.op("dve", lambda e: e.scalar_tensor_tensor(out=o[:, 0:ntok], in0=ps[:, 0:ntok], scalar=GT[:, gcol:gcol + 1],
                                                              in1=rstd[:, 0:ntok], op0=ALU.mult, op1=ALU.mult),
                      reads=[ps.b, GT.b, rstd.b], writes=[o.b])
                kb.dma("pool", QK[qkid * 128:(qkid + 1) * 128, t0:t0 + ntok], o[:, 0:ntok], [o.b], [QKb[qkid]], o.b)
            else:
                kb.op("dve", lambda e: e.scalar_tensor_tensor(out=dst_sb[:, dst_off:dst_off + ntok], in0=ps[:, 0:ntok],
                                                              scalar=GT[:, gcol:gcol + 1], in1=rstd[:, 0:ntok],
                                                              op0=ALU.mult, op1=ALU.mult),
                      reads=[ps.b, GT.b, rstd.b], writes=[dst_sb.b])
        defer(tail)

    def epi_v(ps, vid, t0, ntok=TS, dst_sb=None, dst_off=0):
        vb = bst.next()
        kb.op("act", lambda e: e.activation(out=vb[:, 0:ntok], in_=ps[:, 0:ntok], func=AF.Copy), reads=[ps.b], writes=[vb.b])
        nch = ntok // 128

        def tail():
            PSB = psb_h["t"]
            def fn(e):
                ins = None
                for j in range(nch):
                    ins = e.transpose(PSB[:, j * 128:(j + 1) * 128], vb[:, j * 128:(j + 1) * 128], ident[:])
                return ins
            kb.op("pe", fn, reads=[vb.b, ident.b], writes=[PSB.b])
            if dst_sb is None:
                o = bst.next()
                kb.op("dve", lambda e: e.tensor_copy(out=o[:, 0:ntok], in_=PSB[:, 0:ntok]), reads=[PSB.b], writes=[o.b])
                ch0 = t0 // 128
                kb.dma("pool", VS[vid * 128:(vid + 1) * 128, ch0 * 128:ch0 * 128 + ntok], o[:, 0:ntok], [o.b], [VSb[vid]], o.b)
            else:
                kb.op("dve", lambda e: e.tensor_copy(out=dst_sb[:, dst_off:dst_off + ntok], in_=PSB[:, 0:ntok]),
                      reads=[PSB.b], writes=[dst_sb.b])
        defer(tail)

    def attn_qtile(q_parts, k_parts, kcs, v_of, nv, scale, o_ps, den_ps, ptr, zr, st_ps, nq=TS, coef=None):
        n = len(kcs)
        pts = [None] * n

        def qk(i):
            kc, bias, bbuf, bcol, cf = kcs[i]
            cf = coef if cf is None else cf
            st = st_ps[i % len(st_ps)]
            kp = k_parts(kc)
            pairs = []
            reads = []
            for (kt, koff, npart), (qt_, qoff, _) in zip(kp, q_parts):
                pairs.append((kt[0:npart, koff:koff + 128], qt_[0:npart, qoff:qoff + nq]))
                reads += [kt.b, qt_.b]
            mm_group(st[:, 0:nq], st.b, pairs, reads)
            src = st
            if bias is not None:
                z = zr.next()
                kb.op("dve", lambda e: e.scalar_tensor_tensor(out=z[:, 0:nq], in0=bias, scalar=cf, in1=st[:, 0:nq],
                                                              op0=ALU.mult, op1=ALU.add),
                      reads=[st.b, bbuf], writes=[z.b])
                src = z
            pt = ptr.next()
            kb.op("act", lambda e: e.activation(out=pt[:, 0:nq], in_=src[:, 0:nq], func=AF.Exp, bias=bcol, scale=scale),
                  reads=[src.b, bm.b], writes=[pt.b])
            pts[i] = pt

        def pv(i):
            kc = kcs[i][0]
            pt = pts[i]
            for v in range(nv):
                vt, vap = v_of(kc, v)
                kb.op("pe", lambda e: e.matmul(o_ps[v][:, 0:nq], vap, pt[:, 0:nq], start=(i == 0), stop=(i == n - 1)),
                      reads=[vt.b, pt.b], writes=[o_ps[v].b])
            kb.op("pe", lambda e: e.matmul(den_ps[:, 0:nq], onesv[:], pt[:, 0:nq], start=(i == 0), stop=(i == n - 1)),
                  reads=[onesv.b, pt.b], writes=[den_ps.b])

        LA = min(3, len(st_ps) - 1) if len(st_ps) > 1 else 0
        for i in range(min(LA, n)):
            qk(i)
        for i in range(n):
            if i + LA < n:
                qk(i + LA)
            pv(i)
        pump(cur.get("arate", 0))

    sl_dil = alibi(c.H); sl_win = alibi(c.H); sl_dif = alibi(c.DH)
    SC128 = 128.0 ** -0.5
    SCMLA = 3.0 / math.sqrt(192.0)
    NQT = TOK // TS
    hb = c.HALF // 128

    for l in range(L):
        sp = specs[l]
        g0 = l * c.NG
        xin, xinb = X[4 * l], Xb[4 * l]
        x1, x1b = X[4 * l + 1], Xb[4 * l + 1]
        x2, x2b = X[4 * l + 2], Xb[4 * l + 2]
        x2h, x2hb = X[4 * l + 3], Xb[4 * l + 3]
        x3, x3b = X[4 * l + 4], Xb[4 * l + 4]

        if l > 0:
            ensure(l, order0)
        sA = Scope(kb)
        cur["rate"] = 1
        psb_h["t"] = sA.T("psb", [128, 1024], BF16, psum=True)
        cur["hT"] = sA.T("hT", [128, NC * TS], BF16)
        cur["wring"] = sA.ring("wr", 3, [128, WSLOT], BF16)
        hT = cur["hT"]
        nql = c.QL // 128; nkl = c.KVL // 128
        qlat_f = sA.T("qlat_f", [128, nql * TS], F32)
        qlatN = sA.T("qlatN", [128, nql * TS], BF16)
        kvlat_f = sA.T("kvlat_f", [128, nkl * TS], F32)
        kvN = sA.T("kvN", [128, nkl * TS], BF16)
        kpe = sA.T("kpe", [64, TS], F32)
        sqkpe = sA.T("sqkpe", [64, TS], BF16)
        qr = sA.T("qr", [64, TS], F32)
        rt1 = sA.T("rt1", [64, TS], F32)
        rt2 = sA.T("rt2", [64, TS], F32)
        cosS = sA.T("cosS", [64, TS], F32)
        sinS = sA.T("sinS", [64, TS], F32)
        sqn = sA.T("sqn", [128, TS], BF16)
        sqrr = sA.T("sqrr", [64, TS], BF16)

        def rope(ps_r, ps_s, dst):
            kb.op("dve", lambda e: e.tensor_tensor(out=rt1[:, :], in0=ps_r[0:64, :], in1=cosS[:, :], op=ALU.mult),
                  reads=[ps_r.b, cosS.b], writes=[rt1.b])
            kb.op("dve", lambda e: e.tensor_tensor(out=rt2[:, :], in0=ps_s[0:64, :], in1=sinS[:, :], op=ALU.mult),
                  reads=[ps_s.b, sinS.b], writes=[rt2.b])
            kb.op("pool", lambda e: e.tensor_tensor(out=dst[:, :], in0=rt1[:, :], in1=rt2[:, :], op=ALU.add),
                  reads=[rt1.b, rt2.b], writes=[dst.b])

        cosb = Buf("cosd"); sinb = Buf("sind")
        for st in range(c.NST):
            t0 = st * TS
            if l == 0:
                norm_stream(xin, xinb[st], t0, TS, g0 + c.g_lnmix)
                hsave(0, st)
            else:
                hload(l, st, True)
            kb.dma("pool", cosS[:, :], cos_d[:, t0:t0 + TS], [cosb], [cosS.b], cosS.b)
            kb.dma("pool", sinS[:, :], sin_d[:, t0:t0 + TS], [sinb], [sinS.b], sinS.b)
            wa = sp["wa"]
            for j, blk in enumerate(wa.blocks):
                slot, bw, kc = wload(wa, j)
                for si, kind in enumerate(wa.kinds[j]):
                    if kind[0] == "kpe":
                        pr, psn = ybank(), ybank()
                        dense(slot, bw, kc, 0, 64, hT, TS, TS, pr)
                        dense(slot, bw, kc, 64, 64, hT, TS, TS, psn)
                        rope(pr, psn, kpe)
                        kb.op("act", lambda e: e.activation(out=sqkpe[:, :], in_=kpe[:, :], func=AF.Square),
                              reads=[kpe.b], writes=[sqkpe.b])
                        continue
                    ps = ybank()
                    dense(slot, bw, kc, si * 128, 128, hT, TS, TS, ps)
                    if kind[0] in ("latq", "latkv"):
                        i = kind[1]
                        lf, nl, od = (qlat_f, nql, ones[c.QL]) if kind[0] == "latq" else (kvlat_f, nkl, ones[c.KVL])
                        kb.op("act", lambda e: e.activation(out=lf[:, i * TS:(i + 1) * TS], in_=ps[:, :], func=AF.Copy),
                              reads=[ps.b], writes=[lf.b])
                        sq = sqr.next()
                        kb.op("act", lambda e: e.activation(out=sq[:, :], in_=ps[:, :], func=AF.Square), reads=[ps.b], writes=[sq.b])
                        def lat_tail(i=i, nl=nl, od=od, sq=sq, lf=lf, isq=(kind[0] == "latq")):
                            ssl = PS[6]
                            kb.op("pe", lambda e: e.matmul(ssl[:, :], od[:], sq[:, :], start=(i == 0), stop=(i == nl - 1)),
                                  reads=[od.b, sq.b], writes=[ssl.b])
                            if i == nl - 1:
                                rstd_from(ssl, 1)
                                gq = g0 + (c.g_qa if isq else c.g_kva)
                                dstN = qlatN if isq else kvN
                                for ii in range(nl):
                                    kb.op("dve", lambda e: e.scalar_tensor_tensor(out=dstN[:, ii * TS:(ii + 1) * TS], in0=lf[:, ii * TS:(ii + 1) * TS],
                                                                                  scalar=GT[:, gq + ii:gq + ii + 1], in1=rstd[:, :],
                                                                                  op0=ALU.mult, op1=ALU.mult),
                                          reads=[lf.b, GT.b, rstd.b], writes=[dstN.b])
                        defer(lat_tail)
                    elif kind[0] == "hn":
                        epi_hn(ps, g0 + kind[2], kind[1], t0)
                    elif kind[0] == "v":
                        epi_v(ps, kind[1], t0)
            run_deferred()
            for h in range(c.H):
                slot, bw, kc = wload(sp["wqu"], h)
                pn = ybank(); pr, psn = PS[5], PS[6]
                dense(slot, bw, kc, 0, 128, qlatN, TS, TS, pn)
                dense(slot, bw, kc, 128, 64, qlatN, TS, TS, pr)
                dense(slot, bw, kc, 192, 64, qlatN, TS, TS, psn)
                rope(pr, psn, qr)
                kb.op("act", lambda e: e.activation(out=sqn[:, :], in_=pn[:, :], func=AF.Square), reads=[pn.b], writes=[sqn.b])
                kb.op("act", lambda e: e.activation(out=sqrr[:, :], in_=qr[:, :], func=AF.Square), reads=[qr.b], writes=[sqrr.b])
                def q_tail(h=h, pn=pn, t0=t0):
                    ssp = PS[4]
                    mm_group(ssp[:, :], ssp.b, [(ones[64][:], sqn[:, :]), (ones[64][0:64, :], sqrr[:, :])], [ones[64].b, sqn.b, sqrr.b])
                    rstd_from(ssp, 3)
                    o = bst.next()
                    kb.op("dve", lambda e: e.scalar_tensor_tensor(out=o[:, :], in0=pn[:, :], scalar=GT[:, g0 + c.g_mla:g0 + c.g_mla + 1],
                                                                  in1=rstd[:, :], op0=ALU.mult, op1=ALU.mult),
                          reads=[pn.b, GT.b, rstd.b], writes=[o.b])
                    qid = c.q_mla_n + h
                    kb.dma("pool", QK[qid * 128:(qid + 1) * 128, t0:t0 + TS], o[:, :], [o.b], [QKb[qid]], o.b)
                    o2 = bst.next()
                    kb.op("dve", lambda e: e.scalar_tensor_tensor(out=o2[0:64, :], in0=qr[:, :], scalar=GT[0:64, g0 + c.g_mla + 1:g0 + c.g_mla + 2],
                                                                  in1=rstd[0:64, :], op0=ALU.mult, op1=ALU.mult),
                          reads=[qr.b, GT.b, rstd.b], writes=[o2.b])
                    qid = c.q_mla_r + h
                    kb.dma("pool", QK[qid * 128:qid * 128 + 64, t0:t0 + TS], o2[0:64, :], [o2.b], [QKb[qid]], o2.b)
                defer(q_tail)
            for h in range(c.H):
                slot, bw, kc = wload(sp["wkvu"], h)
                pn = ybank(); pv_ = ybank()
                dense(slot, bw, kc, 0, 128, kvN, TS, TS, pn)
                dense(slot, bw, kc, 128, 128, kvN, TS, TS, pv_)
                epi_v(pv_, c.v_mla + h, t0)
                kb.op("act", lambda e: e.activation(out=sqn[:, :], in_=pn[:, :], func=AF.Square), reads=[pn.b], writes=[sqn.b])
                def k_tail(h=h, pn=pn, t0=t0):
                    ssp = PS[4]
                    mm_group(ssp[:, :], ssp.b, [(ones[64][:], sqn[:, :]), (ones[64][0:64, :], sqkpe[:, :])], [ones[64].b, sqn.b, sqkpe.b])
                    rstd_from(ssp, 3)
                    o = bst.next()
                    kb.op("dve", lambda e: e.scalar_tensor_tensor(out=o[:, :], in0=pn[:, :], scalar=GT[:, g0 + c.g_mla + 2:g0 + c.g_mla + 3],
                                                                  in1=rstd[:, :], op0=ALU.mult, op1=ALU.mult),
                          reads=[pn.b, GT.b, rstd.b], writes=[o.b])
                    qid = c.k_mla_n + h
                    kb.dma("pool", QK[qid * 128:(qid + 1) * 128, t0:t0 + TS], o[:, :], [o.b], [QKb[qid]], o.b)
                    o2 = bst.next()
                    kb.op("dve", lambda e: e.scalar_tensor_tensor(out=o2[0:64, :], in0=kpe[:, :], scalar=GT[0:64, g0 + c.g_mla + 3:g0 + c.g_mla + 4],
                                                                  in1=rstd[0:64, :], op0=ALU.mult, op1=ALU.mult),
                          reads=[kpe.b, GT.b, rstd.b], writes=[o2.b])
                    qid = c.k_mla_r + h
                    kb.dma("pool", QK[qid * 128:qid * 128 + 64, t0:t0 + TS], o2[0:64, :], [o2.b], [QKb[qid]], o2.b)
                defer(k_tail)
            run_deferred()
        sA.close()

        ST_PS = [PS[0], PS[1], PS[6]]

        def attn_scope(names2):
            sB = Scope(kb)
            NH = 2
            tiles = {nm: [sB.T(f"{nm}{i}", [128, TOK], BF16) for i in range(NH)] for nm in names2}
            ptr = sB.ring("pt", 5, [128, TS], BF16)
            zr = sB.ring("z", 5, [128, TS], F32)
            rec = sB.T("rec", [128, TS], F32)
            ps7 = sB.T("ps7", [128, 512], F32, psum=True)
            ST_PS[:] = [PS[0], PS[1], PS[6], ps7]
            return sB, tiles, ptr, zr, rec

        hcnt = [0]
        def load_head(tiles, ids):
            i = hcnt[0] % 2; hcnt[0] += 1
            out = {}
            for nm, (kind, sid, rows) in ids.items():
                tl = tiles[nm][i]
                if kind == "qk":
                    kb.dma("sp", tl[0:rows, :], QK[sid * 128:sid * 128 + rows, :], [QKb[sid]], [tl.b], tl.b)
                else:
                    kb.dma("sp", tl[:, :], VS[sid * 128:(sid + 1) * 128, :], [VSb[sid]], [tl.b], tl.b)
                out[nm] = tl
            return out

        def finalize_store(rec, o_ps_list, den_ps, chan_blocks, t0, sinkcol=None):
            if sinkcol is not None:
                recip_act(rec, den_ps, bias_ap=sinkcol, extra=[esink.b])
            else:
                recip_act(rec, den_ps)
            for ops, cb in zip(o_ps_list, chan_blocks):
                o = bst.next()
                kb.op("dve", lambda e: e.tensor_tensor(out=o[:, :], in0=ops[:, :], in1=rec[:, :], op=ALU.mult),
                      reads=[ops.b, rec.b], writes=[o.b])
                kb.dma("pool", OT[cb * 128:(cb + 1) * 128, t0:t0 + TS], o[:, :], [o.b], [OTb[cb]], o.b)

        cur["arate"] = 2
        OD = [(PS[2], PS[3]), (PS[4], PS[5])]

        sB, tl, ptr, zr, rec = attn_scope(["qn", "kn", "q2", "k2", "v"])
        for h in range(c.H):
            hd = load_head(tl, {"qn": ("qk", c.q_mla_n + h, 128), "q2": ("qk", c.q_mla_r + h, 64),
                                "kn": ("qk", c.k_mla_n + h, 128), "k2": ("qk", c.k_mla_r + h, 64),
                                "v": ("v", c.v_mla + h, 128)})
            for qt in range(NQT):
                qh = (qt * TS) // c.HALF
                ops_, dps_ = OD[qt % 2]
                kcs = [(kc, None, None, (zcol if (kc // hb) == qh else mcol), None) for kc in range(c.NCH)]
                attn_qtile([(hd["qn"], qt * TS, 128), (hd["q2"], qt * TS, 64)],
                           lambda kc: [(hd["kn"], kc * 128, 128), (hd["k2"], kc * 128, 64)],
                           kcs, lambda kc, v: (hd["v"], hd["v"][:, kc * 128:(kc + 1) * 128]), 1, SCMLA,
                           [ops_], dps_, ptr, zr, ST_PS)
                finalize_store(rec, [ops_], dps_, [0 * c.H + h], qt * TS)
        sB.close()

        sB, tl, ptr, zr, rec = attn_scope(["qn", "kn", "v"])
        tdl = sB.T("tdl", [128, WD], F32); tlm = sB.T("tlm", [128, WL], F32)
        tb1 = Buf("tabd"); tb2 = Buf("tabl")
        kb.dma("sp", tdl[:, :], tabd_d[:, :], [tb1], [tdl.b], tdl.b)
        kb.dma("sp", tlm[:, :], tabl_d[:, :], [tb2], [tlm.b], tlm.b)
        tcomb = [sB.T(f"tcomb{i}", [128, WL], F32) for i in range(2)]
        for h in range(c.H):
            hd = load_head(tl, {"qn": ("qk", c.q_dil + h, 128), "kn": ("qk", c.k_dil + h, 128), "v": ("v", c.v_dil + h, 128)})
            coef = sl_dil[h] / SC128
            tcb = tcomb[h % 2]
            kb.op("dve", lambda e: e.scalar_tensor_tensor(out=tcb[:, :], in0=tdl[:, 768:768 + WL], scalar=coef, in1=tlm[:, :],
                                                          op0=ALU.mult, op1=ALU.add),
                  reads=[tdl.b, tlm.b], writes=[tcb.b])
            for qt in range(NQT):
                qh = (qt * TS) // c.HALF
                ops_, dps_ = OD[qt % 2]
                kcs = []
                for rel in range(-8, 12):
                    kc = qt * 4 + rel
                    if kc < 0 or kc >= c.NCH:
                        continue
                    bc_ = (zcol if (kc // hb) == qh else mcol)
                    if -2 <= rel <= 5:
                        kcs.append((kc, tcb[:, (5 - rel) * 128:(5 - rel) * 128 + TS], tcb.b, bc_, 1.0))
                    else:
                        kcs.append((kc, tdl[:, (11 - rel) * 128:(11 - rel) * 128 + TS], tdl.b, bc_, None))
                attn_qtile([(hd["qn"], qt * TS, 128)], lambda kc: [(hd["kn"], kc * 128, 128)],
                           kcs, lambda kc, v: (hd["v"], hd["v"][:, kc * 128:(kc + 1) * 128]), 1, SC128,
                           [ops_], dps_, ptr, zr, ST_PS, coef=coef)
                finalize_store(rec, [ops_], dps_, [1 * c.H + h], qt * TS)
        sB.close()

        sB, tl, ptr, zr, rec = attn_scope(["qn", "kn", "v"])
        twn = sB.T("twn", [128, WW], F32)
        tb3 = Buf("tabw")
        kb.dma("sp", twn[:, :], tabw_d[:, :], [tb3], [twn.b], twn.b)
        for h in range(c.H):
            g = h // (c.H // c.WKV)
            hd = load_head(tl, {"qn": ("qk", c.q_win + h, 128), "kn": ("qk", c.k_win + g, 128), "v": ("v", c.v_win + g, 128)})
            coef = sl_win[h] / SC128
            for qt in range(NQT):
                qh = (qt * TS) // c.HALF
                ops_, dps_ = OD[qt % 2]
                kcs = []
                for rel in range(-1, 5):
                    kc = qt * 4 + rel
                    if kc < 0 or kc >= c.NCH:
                        continue
                    kcs.append((kc, twn[:, (4 - rel) * 128:(4 - rel) * 128 + TS], twn.b, (zcol if (kc // hb) == qh else mcol), None))
                attn_qtile([(hd["qn"], qt * TS, 128)], lambda kc: [(hd["kn"], kc * 128, 128)],
                           kcs, lambda kc, v: (hd["v"], hd["v"][:, kc * 128:(kc + 1) * 128]), 1, SC128,
                           [ops_], dps_, ptr, zr, ST_PS, coef=coef)
                finalize_store(rec, [ops_], dps_, [2 * c.H + h], qt * TS, sinkcol=esink[:, l * c.H + h:l * c.H + h + 1])
        sB.close()

        sB, tl, ptr, zr, rec = attn_scope(["qn", "kn", "q2", "k2", "v", "v2"])
        tab_a = sB.T("taba", [128, WA], mybir.dt.int16)
        tb4 = Buf("taba_d")
        kb.dma("sp", tab_a[:, :], taba_d[:, :], [tb4], [tab_a.b], tab_a.b)
        om = [sB.T(f"om{i}", [128, TS], F32) for i in range(4)]
        od_ = [sB.T(f"od{i}", [128, TS], F32) for i in range(2)]
        sqo = [sB.T(f"sqo{i}", [128, TS], BF16) for i in range(2)]
        lam_init = 0.8 - 0.6 * math.exp(-0.3 * l)
        relmax = c.NCH - 1
        for h in range(c.DH):
            hd = load_head(tl, {"qn": ("qk", c.q_dif + 2 * h, 128), "q2": ("qk", c.q_dif + 2 * h + 1, 128),
                                "kn": ("qk", c.k_dif + 2 * h, 128), "k2": ("qk", c.k_dif + 2 * h + 1, 128),
                                "v": ("v", c.v_dif + 2 * h, 128), "v2": ("v", c.v_dif + 2 * h + 1, 128)})
            coef = -sl_dif[h] / SC128
            for qt in range(NQT):
                qh = (qt * TS) // c.HALF
                for m in range(2):
                    qtile = hd["qn"] if m == 0 else hd["q2"]
                    ktile = hd["kn"] if m == 0 else hd["k2"]
                    kcs = []
                    for kc in range(c.NCH):
                        rel = kc - qt * 4
                        u0 = (relmax - rel) * 128
                        kcs.append((kc, tab_a[:, u0:u0 + TS], tab_a.b, (zcol if (kc // hb) == qh else mcol), None))
                    attn_qtile([(qtile, qt * TS, 128)], lambda kc, ktile=ktile: [(ktile, kc * 128, 128)],
                               kcs, lambda kc, v: ((hd["v"], hd["v"][:, kc * 128:(kc + 1) * 128]) if v == 0 else
                                                   (hd["v2"], hd["v2"][:, kc * 128:(kc + 1) * 128])), 2, SC128,
                               [PS[2], PS[3]], PS[4], ptr, zr, ST_PS, coef=coef)
                    recip_act(rec, PS[4])
                    for v in range(2):
                        kb.op("dve", lambda e: e.tensor_tensor(out=om[m * 2 + v][:, :], in0=PS[2 + v][:, :], in1=rec[:, :], op=ALU.mult),
                              reads=[PS[2 + v].b, rec.b], writes=[om[m * 2 + v].b])
                for v in range(2):
                    kb.op("dve", lambda e: e.scalar_tensor_tensor(out=od_[v][:, :], in0=om[2 + v][:, :], scalar=lam[:, 4 * l + 1:4 * l + 2],
                                                                  in1=om[v][:, :], op0=ALU.mult, op1=ALU.add),
                          reads=[om[2 + v].b, om[v].b, lam.b], writes=[od_[v].b])
                    kb.op("act", lambda e: e.activation(out=sqo[v][:, :], in_=od_[v][:, :], func=AF.Square), reads=[od_[v].b], writes=[sqo[v].b])
                ssp = PS[5]
                mm_group(ssp[:, :], ssp.b, [(ones[256][:], sqo[0][:, :]), (ones[256][:], sqo[1][:, :])], [ones[256].b, sqo[0].b, sqo[1].b])
                rstd_from(ssp, 1)
                for v in range(2):
                    o = bst.next()
                    gc = g0 + c.g_subln + v
                    kb.op("dve", lambda e: e.scalar_tensor_tensor(out=od_[v][:, :], in0=od_[v][:, :], scalar=GT[:, gc:gc + 1],
                                                                  in1=rstd[:, :], op0=ALU.mult, op1=ALU.mult),
                          reads=[od_[v].b, GT.b, rstd.b], writes=[od_[v].b])
                    kb.op("act", lambda e: e.activation(out=o[:, :], in_=od_[v][:, :], func=AF.Copy, scale=(1.0 - lam_init)),
                          reads=[od_[v].b], writes=[o.b])
                    cb = 3 * c.H + 2 * h + v
                    kb.dma("pool", OT[cb * 128:(cb + 1) * 128, qt * TS:(qt + 1) * TS], o[:, :], [o.b], [OTb[cb]], o.b)
        sB.close()

        ensure(l, order1)
        sC = Scope(kb)
        cur["arate"] = 0
        psb_h["t"] = sC.T("psb", [128, 1024], BF16, psum=True)
        cur["hT"] = sC.T("hT", [128, NC * TS], BF16)
        cur["wring"] = sC.ring("wr", 3, [128, WSLOT], BF16)
        hT = cur["hT"]
        nblk_o = 4 * c.BW // 128
        bigw = max(nblk_o, c.FCH, NC) * TS
        big = sC.T("big", [128, bigw], BF16)
        macc = sC.T("macc", [128, TS], F32)
        mtmp = sC.T("mtmp", [128, TS], F32)
        sg_r = sC.ring("sg", 2, [128, TS], F32)
        qm = sC.T("qm", [128, c.MEMH * TS], BF16)
        om_ = sC.T("omem", [128, c.MEMH * TS], BF16)
        recm = [sC.T(f"recm{i}", [128, TS], F32) for i in range(2)]
        ptm = sC.ring("ptm", 2, [128, TS], BF16)
        mgb = [Buf(f"mg{l}_{st}", True) for st in range(c.NST)]

        NM = 2 * c.MEM
        memb = Buf(f"memT{l}")
        norm_stream(memT, memb, 0, NM, g0 + c.g_memln)
        for j in range(len(sp["mkv"].blocks)):
            slot, bw, kc = wload(sp["mkv"], j)
            for si in range(2):
                sb_ = 2 * j + si
                ps = ybank()
                dense(slot, bw, kc, si * 128, 128, hT, TS, NM, ps)
                if sb_ < c.MEMH:
                    epi_hn(ps, g0 + c.g_mem + 1, None, 0, ntok=NM, dst_sb=KTm, dst_off=sb_ * 512)
                else:
                    epi_v(ps, None, 0, ntok=NM, dst_sb=Vm, dst_off=(sb_ - c.MEMH) * 512)
        run_deferred()

        def residual_out(ps, xsrc, xsb, xdst, xdb, ob, t0, fuse=None):
            xt = xs.next()
            kb.dma("pool", xt[:, :], xsrc[ob * 128:(ob + 1) * 128, t0:t0 + TS], [xsb], [xt.b], xt.b)
            o = ost.next()
            kb.op("dve", lambda e: e.tensor_tensor(out=o[:, :], in0=ps[:, :], in1=xt[:, :], op=ALU.add),
                  reads=[ps.b, xt.b], writes=[o.b])
            kb.dma("pool", xdst[ob * 128:(ob + 1) * 128, t0:t0 + TS], o[:, :], [o.b], [xdb], o.b)
            if fuse is None:
                return
            gcol0, mode, st_ = fuse
            hT_ = cur["hT"]
            sq = sqr.next()
            kb.op("act", lambda e: e.activation(out=sq[:, :], in_=o[:, :], func=AF.Square), reads=[o.b], writes=[sq.b])
            ssn = PS[4]

            def acc_tail(sq=sq, ob=ob):
                kb.op("pe", lambda e: e.matmul(ssn[:, :], ones[c.D][:], sq[:, :], start=(ob == 0), stop=(ob == NC - 1)),
                      reads=[ones[c.D].b, sq.b], writes=[ssn.b])
            defer(acc_tail)
            gc = GT[:, gcol0 + ob:gcol0 + ob + 1]
            if mode == "hT":
                kb.op("act", lambda e: e.activation(out=hT_[:, ob * TS:(ob + 1) * TS], in_=o[:, :], func=AF.Copy, scale=gc),
                      reads=[o.b, GT.b], writes=[hT_.b])
            else:
                ob_ = bst.next()
                kb.op("act", lambda e: e.activation(out=ob_[:, :], in_=o[:, :], func=AF.Copy, scale=gc),
                      reads=[o.b, GT.b], writes=[ob_.b])
                kb.dma("pool", HG[l + 1][ob * 128:(ob + 1) * 128, t0:t0 + TS], ob_[:, :], [ob_.b], [HGb[l + 1][st_]], ob_.b)
            if ob == NC - 1:
                run_deferred()
                rstd_from(ssn, 1)
                if mode == "hT":
                    hscale(rstd)
                else:
                    kb.dma("pool", RS[l + 1][st_ * 128:(st_ + 1) * 128, :], rstd[:, :], [rstd.b], [RSb[l + 1][st_]], rstd.b)

        kbw = c.BW // 128
        for st in range(c.NST):
            t0 = st * TS
            hload(l, st, l > 0)
            kb.dma("sp", big[:, 0:nblk_o * TS].rearrange("p (b t) -> p b t", b=nblk_o),
                   OT[:, t0:t0 + TS].rearrange("(b p) t -> p b t", p=128), OTb, [big.b], big.b)
            for ob in range(NC):
                gs = [wload(sp["wg"], 2 * ob + pr) for pr in range(2)]
                bs, bbw, bkc = wload(sp["wb"], ob)
                for i in range(4):
                    gslot, gbw, gkc = gs[i // 2]
                    pg = ybank(); pp = ybank()
                    dense(gslot, gbw, gkc, (i % 2) * 128, 128, hT, TS, TS, pg)
                    dense(bs, bbw, bkc, i * 128, 128, big, TS, TS, pp, rhs_off=i * kbw * TS)
                    sg = sg_r.next()
                    kb.op("act", lambda e: e.activation(out=sg[:, :], in_=pg[:, :], func=AF.Sigmoid), reads=[pg.b], writes=[sg.b])
                    if i == 0:
                        kb.op("dve", lambda e: e.tensor_tensor(out=macc[:, :], in0=pp[:, :], in1=sg[:, :], op=ALU.mult),
                              reads=[pp.b, sg.b], writes=[macc.b])
                    else:
                        kb.op("dve", lambda e: e.tensor_tensor(out=mtmp[:, :], in0=pp[:, :], in1=sg[:, :], op=ALU.mult),
                              reads=[pp.b, sg.b], writes=[mtmp.b])
                        if i < 3:
                            kb.op("pool", lambda e: e.tensor_tensor(out=macc[:, :], in0=macc[:, :], in1=mtmp[:, :], op=ALU.add),
                                  reads=[macc.b, mtmp.b], writes=[macc.b])
                        else:
                            o = bst.next()
                            kb.op("pool", lambda e: e.tensor_tensor(out=o[:, :], in0=macc[:, :], in1=mtmp[:, :], op=ALU.add),
                                  reads=[macc.b, mtmp.b], writes=[o.b])
                            kb.dma("pool", MG[ob * 128:(ob + 1) * 128, t0:t0 + TS], o[:, :], [o.b], [mgb[st]], o.b)
            kb.dma("sp", big[:, 0:NC * TS].rearrange("p (b t) -> p b t", b=NC),
                   MG[:, t0:t0 + TS].rearrange("(b p) t -> p b t", p=128), [mgb[st]], [big.b], big.b)
            for j in range(len(sp["wo"].blocks)):
                slot, bw, kc = wload(sp["wo"], j)
                for si in range(2):
                    ps = ybank()
                    dense(slot, bw, kc, si * 128, 128, big, TS, TS, ps)
                    residual_out(ps, xin, xinb[st], x1, x1b[st], 2 * j + si, t0, fuse=(g0 + c.g_lnmem, "hT", st))
            slot_id = t0 // c.HALF
            for j in range(len(sp["mq"].blocks)):
                slot, bw, kc = wload(sp["mq"], j)
                for si in range(2):
                    hh = 2 * j + si
                    ps = ybank()
                    dense(slot, bw, kc, si * 128, 128, hT, TS, TS, ps)
                    epi_hn(ps, g0 + c.g_mem, None, 0, dst_sb=qm, dst_off=hh * TS)
            run_deferred()
            for hh in range(c.MEMH):
                nmc = c.MEM // 128
                kcs = [(kc, None, None, zcol, None) for kc in range(nmc)]
                base = hh * 512 + slot_id * c.MEM
                ops_, dps_ = (PS[6], PS[5]) if hh % 2 == 0 else (PS[2], PS[3])
                attn_qtile([(qm, hh * TS, 128)],
                           lambda kc: [(KTm, base + kc * 128, 128)],
                           kcs, lambda kc, v: (Vm, Vm[:, base + kc * 128: base + (kc + 1) * 128]),
                           1, SC128, [ops_], dps_, ptm, None, [PS[4], PS[0]])
                rcm = recm[hh % 2]
                recip_act(rcm, dps_)
                kb.op("dve", lambda e: e.tensor_tensor(out=om_[:, hh * TS:(hh + 1) * TS], in0=ops_[:, :], in1=rcm[:, :], op=ALU.mult),
                      reads=[ops_.b, rcm.b], writes=[om_.b])
            for j in range(len(sp["mo"].blocks)):
                slot, bw, kc = wload(sp["mo"], j)
                for si in range(2):
                    ps = ybank()
                    dense(slot, bw, kc, si * 128, 128, om_, TS, TS, ps)
                    residual_out(ps, x1, x1b[st], x2, x2b[st], 2 * j + si, t0, fuse=(g0 + c.g_lnffn, "hT", st))
            for hf in range(2):
                k0 = hf * c.FCH
                nk = min(c.FCH, c.FC - k0)
                for jj in range(nk):
                    slot, bw, kc = wload(sp["fi"], k0 + jj)
                    pg = ybank(); pu = ybank()
                    dense(slot, bw, kc, 0, 128, hT, TS, TS, pg)
                    dense(slot, bw, kc, 128, 128, hT, TS, TS, pu)
                    sg = sg_r.next()
                    kb.op("act", lambda e: e.activation(out=sg[:, :], in_=pg[:, :], func=AF.Silu), reads=[pg.b], writes=[sg.b])
                    kb.op("dve", lambda e: e.tensor_tensor(out=big[:, jj * TS:(jj + 1) * TS], in0=pu[:, :], in1=sg[:, :], op=ALU.mult),
                          reads=[pu.b, sg.b], writes=[big.b])
                xs_, xsb_ = (x2, x2b[st]) if hf == 0 else (x2h, x2hb[st])
                xd_, xdb_ = (x2h, x2hb[st]) if hf == 0 else (x3, x3b[st])
                for ob in range(NC):
                    slot, bw, kc = wload(sp["fo"], hf * NC + ob)
                    ps = ybank()
                    dense(slot, bw, kc, 0, 128, big, TS, TS, ps)
                    fz = ((l + 1) * c.NG + c.g_lnmix, "dram", st) if (hf == 1 and l < L - 1) else None
                    residual_out(ps, xs_, xsb_, xd_, xdb_, ob, t0, fuse=fz)
        sC.close()

    kb.barrier()
    kb.es.close()
    return nc


def tab_widths(c):
    WW = 5 * 128 + 512
    WD = 19 * 128 + 512
    WL = 7 * 128 + 512
    relmax = c.NCH - 1; relmin = -(c.NCH - 4)
    WA = (relmax - relmin) * 128 + 512
    return WW, WD, WL, WA


def make_tables(c):
    WW, WD, WL, WA = tab_widths(c)
    i = np.arange(128)[:, None].astype(np.float64)
    sc = 128.0 ** -0.5

    def wide(width, relmax, f):
        uu = np.arange(width)[None, :].astype(np.float64)
        return f(i - uu + relmax * 128)

    def f_win(d):
        return np.where(np.abs(d) <= 128, -np.abs(d), NEG)

    def mult_of(d):
        ad = np.abs(d)
        return ((ad <= 64).astype(np.float64) + ((ad <= 256) & (np.mod(d, 4) == 0))
                + ((ad <= 1024) & (np.mod(d, 16) == 0)))

    def f_dil(d):
        return np.where(mult_of(d) > 0, -np.abs(d), NEG)

    def f_lm(d):
        m = mult_of(d)
        return np.where(m > 1, np.log(np.maximum(m, 1.0)), 0.0) / sc

    t_win = wide(WW, 4, f_win).astype(np.float32)
    t_dil = wide(WD, 11, f_dil).astype(np.float32)
    t_lm = wide(WL, 5, f_lm).astype(np.float32)
    t_abs = wide(WA, c.NCH - 1, np.abs).astype(np.int16)
    return t_win, t_dil, t_lm, t_abs


def make_gt(c, p, l):
    cols = []
    def chunks(v):
        v = np.asarray(v, np.float32).reshape(-1, 128)
        for r in v:
            cols.append(r)
    def col64(v):
        z = np.zeros(128, np.float32); z[:64] = v; cols.append(z)
    chunks(p["ln_mix_g"][l]); chunks(p["ln_mem_g"][l]); chunks(p["mem_ln_g"][l]); chunks(p["ln_ffn_g"][l])
    chunks(p["mla_qa_g"][l]); chunks(p["mla_kva_g"][l])
    g = p["mla_qk_g"][l]
    cols.append(g[0, :128]); col64(g[0, 128:]); cols.append(g[1, :128]); col64(g[1, 128:])
    chunks(p["dil_qk_g"][l]); chunks(p["win_qk_g"][l]); chunks(p["diff_qk_g"][l]); chunks(p["mem_qk_g"][l])
    chunks(p["diff_subln_g"][l]); chunks(p["diff_lambda"][l])
    for h in range(c.H):
        cols.append(np.full(128, p["win_sink"][l][h], np.float32))
    gt = np.stack(cols, axis=1).astype(np.float32)
    assert gt.shape == (128, c.NG), gt.shape
    return gt


def rope_tables(pos):
    half = 32
    inv = (10000.0 ** (-(np.arange(half, dtype=np.float32)) / half)).astype(np.float32)
    ang = (pos.astype(np.float32)[None, :] * inv[:, None]).astype(np.float32).astype(np.float64)
    cs = np.cos(ang); sn = np.sin(ang)
    cosT = np.concatenate([cs, cs], axis=0).astype(np.float32)
    sinT = np.concatenate([-sn, sn], axis=0).astype(np.float32)
    return cosT, sinT


_CACHE = {}


def run(c, inputs, debug=False):
    p = {k: np.asarray(v) for k, v in inputs.items()}
    xp, xsm = p["x_prompt"], p["x_sample"]
    mp, ms = p["mem_prompt"], p["mem_sample"]
    nP = xp.shape[0]
    S = xp.shape[1]; S2 = xsm.shape[1]
    assert S == c.TOK and 2 * S2 == c.TOK
    key = (c.D, c.TOK, debug)
    if key not in _CACHE:
        _CACHE[key] = build(c, debug)
    nc = _CACHE[key]
    t_win, t_dil, t_lm, t_abs = make_tables(c)
    gt = np.concatenate([make_gt(c, p, l) for l in range(c.depth)], axis=1)
    ident = np.eye(128, dtype=np.float32)
    cos_p, sin_p = rope_tables(np.arange(S))
    pos_s = np.concatenate([np.arange(S2), np.arange(S2)])
    cos_s, sin_s = rope_tables(pos_s)
    wnames = ["w_in", "mla_wq_up", "mla_wkv_up", "w_branch", "w_out", "mem_wq", "mem_wkv", "mem_wo", "ffn_w_in", "ffn_w_out"]
    shared = {k: np.ascontiguousarray(p[k], dtype=np.float32) for k in wnames}
    shared.update({"gt": gt, "tab_win": t_win, "tab_dil": t_dil, "tab_lm": t_lm, "tab_abs": t_abs, "ident_in": ident})
    in_maps = []
    ncores = nP + xsm.shape[0] // 2
    for core in range(ncores):
        m = dict(shared)
        if core < nP:
            m["xT"] = np.ascontiguousarray(xp[core].T)
            mt = mp[core].T
            m["memT"] = np.ascontiguousarray(np.concatenate([mt, mt], axis=1))
            m["cosT"], m["sinT"] = cos_p, sin_p
            m["bmask"] = np.zeros((128, 2), np.float32)
        else:
            b0 = 2 * (core - nP)
            m["xT"] = np.ascontiguousarray(np.concatenate([xsm[b0].T, xsm[b0 + 1].T], axis=1))
            m["memT"] = np.ascontiguousarray(np.concatenate([ms[b0].T, ms[b0 + 1].T], axis=1))
            m["cosT"], m["sinT"] = cos_s, sin_s
            bmk = np.zeros((128, 2), np.float32); bmk[:, 1] = NEG
            m["bmask"] = bmk
        in_maps.append(m)
    res = run_bass_kernel_spmd(nc, in_maps, core_ids=list(range(ncores)))
    outs = res.results
    yp = np.stack([np.ascontiguousarray(outs[i]["yT"].T) for i in range(nP)], axis=0)
    ysl = []
    for core in range(nP, ncores):
        y = outs[core]["yT"].T
        ysl.append(y[:S2]); ysl.append(y[S2:])
    ys = np.stack([np.ascontiguousarray(a) for a in ysl], axis=0)
    return (yp.astype(np.float32), ys.astype(np.float32)), outs


def kernel(**inputs):
    c = Cfg()
    (yp, ys), _ = run(c, inputs)
    return (yp, ys)
```
